# Optimizing a Trainium2 kernel written in Bass

```python
import math
import jax, jax.numpy as jnp
from jax import lax
import numpy as np

D_MODEL = 2048
BATCH = 16
SEQ = 2048
DEPTH = 1
DEC_BATCH = 32
DEC_SEQ = 64
PAST_LEN = 4096

CHUNK = 64
N_META = 16
META_PAD = (-N_META) % CHUNK
CONV_K = 4
EPS = 1e-6

SSD_D_INNER = D_MODEL
SSD_HEAD_DIM = 64
SSD_HEADS = SSD_D_INNER // SSD_HEAD_DIM
SSD_GROUPS = 4
SSD_HPG = SSD_HEADS // SSD_GROUPS
SSD_STATE = 128
SSD_CONV_CH = SSD_D_INNER + 2 * SSD_GROUPS * SSD_STATE

GDN_HEADS = 16
GDN_HEAD_DIM = D_MODEL // GDN_HEADS
GDN_WIDTH = GDN_HEADS * GDN_HEAD_DIM
GDN_CONV_CH = 3 * GDN_WIDTH

MIX_WIDTH = SSD_D_INNER + GDN_WIDTH
D_FF = 4 * D_MODEL

SPLITS = (SSD_D_INNER, SSD_CONV_CH, SSD_HEADS, GDN_CONV_CH, GDN_WIDTH, GDN_HEADS, GDN_HEADS, D_MODEL, D_MODEL)
IN_COLS = sum(SPLITS)
SPLIT_IDX = tuple(int(v) for v in np.cumsum(SPLITS)[:-1])

kernel_name = "hybrid_ssd_gdn_streaming_step"


def _rmsnorm(x, w):
    xf = x.astype(jnp.float32)
    xf = xf * lax.rsqrt(jnp.mean(xf * xf, axis=-1, keepdims=True) + EPS)
    return (xf * w.astype(jnp.float32)).astype(x.dtype)


def _l2norm(x):
    return x * lax.rsqrt(jnp.sum(x * x, axis=-1, keepdims=True) + EPS)


def _pad_time(a, left, right):
    pads = [(0, 0)] * a.ndim
    pads[1] = (left, right)
    return jnp.pad(a, pads)


def _causal_conv(x, buf, w, b):
    xp = jnp.concatenate([buf.astype(x.dtype), x], axis=1)
    y = lax.conv_general_dilated(xp, w[:, None, :].astype(x.dtype), window_strides=(1,), padding='VALID',
                                 dimension_numbers=('NWC', 'WIO', 'NWC'), feature_group_count=x.shape[-1])
    if b is not None:
        y = y + b.astype(x.dtype)
    return y, xp[:, -(CONV_K - 1):]


def _ssd_scan(x, dt, a, bm, cm, h0):
    b, l = x.shape[:2]
    nc = l // CHUNK

    def blocks(t):
        return jnp.moveaxis(t.reshape(b, nc, CHUNK, *t.shape[2:]), 1, 0)

    causal = jnp.tril(jnp.ones((CHUNK, CHUNK), bool))[None, :, :, None, None]

    def step(h, inp):
        xc, dtc, bc, cc = inp
        acs = jnp.cumsum(dtc * a, axis=1)
        lmat = jnp.exp(jnp.where(causal, acs[:, :, None] - acs[:, None, :], -jnp.inf))
        xdt = xc * dtc[..., None]
        cb = jnp.einsum('blgn,bsgn->blsg', cc, bc)
        y_diag = jnp.einsum('blsg,blsgr,bsgrp->blgrp', cb, lmat, xdt)
        y_off = jnp.einsum('blgn,bgrpn->blgrp', cc, h) * jnp.exp(acs)[..., None]
        h_new = (h * jnp.exp(acs[:, -1])[..., None, None]
                 + jnp.einsum('bsgn,bsgr,bsgrp->bgrpn', bc, jnp.exp(acs[:, -1:] - acs), xdt))
        return h_new, y_diag + y_off

    h_fin, y = lax.scan(step, h0, (blocks(x), blocks(dt), blocks(bm), blocks(cm)))
    y = jnp.moveaxis(y, 0, 1).reshape(x.shape)
    return y, h_fin


def _gdn_scan(q, k, v, g, beta, s0):
    b, h, l = g.shape
    nc = l // CHUNK

    def blocks(t):
        return jnp.moveaxis(t.reshape(b, h, nc, CHUNK, *t.shape[3:]), 2, 0)

    incl = jnp.tril(jnp.ones((CHUNK, CHUNK), bool))
    strict = jnp.tril(jnp.ones((CHUNK, CHUNK), bool), -1)
    eye = jnp.eye(CHUNK, dtype=jnp.float32)

    def step(s, inp):
        qc, kc, vc, gc, bc = inp
        gcum = jnp.cumsum(gc, axis=-1)
        decay = jnp.exp(jnp.where(incl, gcum[..., :, None] - gcum[..., None, :], -jnp.inf))
        lmat = jnp.where(strict, bc[..., :, None] * jnp.einsum('bhtk,bhsk->bhts', kc, kc) * decay, 0.0)
        tinv = lax.linalg.triangular_solve(eye + lmat, jnp.broadcast_to(eye, lmat.shape),
                                           left_side=True, lower=True, unit_diagonal=True)
        u = jnp.einsum('bhts,bhsv->bhtv', tinv, vc * bc[..., None])
        w = jnp.einsum('bhts,bhsk->bhtk', tinv, kc * (bc * jnp.exp(gcum))[..., None])
        wv = u - jnp.einsum('bhtk,bhkv->bhtv', w, s)
        qk = jnp.einsum('bhtk,bhsk->bhts', qc, kc) * decay
        o = (jnp.einsum('bhtk,bhkv->bhtv', qc * jnp.exp(gcum)[..., None], s)
             + jnp.einsum('bhts,bhsv->bhtv', qk, wv))
        s_new = (s * jnp.exp(gcum[..., -1])[..., None, None]
                 + jnp.einsum('bhsk,bhsv->bhkv', kc * jnp.exp(gcum[..., -1:] - gcum)[..., None], wv))
        return s_new, o

    s_fin, o = lax.scan(step, s0, (blocks(q), blocks(k), blocks(v), blocks(g), blocks(beta)))
    o = jnp.moveaxis(o, 0, 2).reshape(b, h, l, -1)
    return o, s_fin


def _layer(x, lpad, conv_a, ssm_a, conv_b, ssm_b,
           norm_mix_w, w_in, ssd_conv_w, ssd_conv_b, ssd_dt_bias, ssd_a_log, ssd_d, ssd_norm_w,
           gdn_conv_w, gdn_dt_bias, gdn_a_log, gdn_norm_w, w_out, norm_mlp_w, w_up, w_down):
    f32 = jnp.float32
    b, l, _ = x.shape
    rpad = (-(lpad + l)) % CHUNK

    def pad(t):
        return _pad_time(t, lpad, rpad)

    u = _rmsnorm(x, norm_mix_w)
    z_a, xbc_a, dt_a, qkv_b, z_b, beta_b, a_b, gate_a, gate_b = jnp.split(u @ w_in, SPLIT_IDX, axis=-1)

    xbc, conv_a_new = _causal_conv(xbc_a, conv_a, ssd_conv_w, ssd_conv_b)
    xbc = jax.nn.silu(xbc).astype(f32)
    xs, bm, cm = jnp.split(xbc, [SSD_D_INNER, SSD_D_INNER + SSD_GROUPS * SSD_STATE], axis=-1)
    xs = xs.reshape(b, l, SSD_GROUPS, SSD_HPG, SSD_HEAD_DIM)
    bm = bm.reshape(b, l, SSD_GROUPS, SSD_STATE)
    cm = cm.reshape(b, l, SSD_GROUPS, SSD_STATE)
    dt = jax.nn.softplus(dt_a.astype(f32) + ssd_dt_bias.astype(f32)).reshape(b, l, SSD_GROUPS, SSD_HPG)
    a = -jnp.exp(ssd_a_log.astype(f32)).reshape(SSD_GROUPS, SSD_HPG)
    h0 = ssm_a.astype(f32).reshape(b, SSD_GROUPS, SSD_HPG, SSD_HEAD_DIM, SSD_STATE)
    ya, ssm_a_new = _ssd_scan(pad(xs), pad(dt), a, pad(bm), pad(cm), h0)
    ya = ya[:, lpad:lpad + l] + xs * ssd_d.astype(f32).reshape(SSD_GROUPS, SSD_HPG, 1)
    ya = ya.reshape(b, l, SSD_D_INNER) * jax.nn.silu(z_a.astype(f32))
    ya = _rmsnorm(ya.reshape(b, l, SSD_GROUPS, -1), ssd_norm_w.reshape(SSD_GROUPS, -1)).reshape(b, l, SSD_D_INNER)

    qkv, conv_b_new = _causal_conv(qkv_b, conv_b, gdn_conv_w, None)
    qkv = jax.nn.silu(qkv).astype(f32).reshape(b, l, 3, GDN_HEADS, GDN_HEAD_DIM)
    q = _l2norm(qkv[:, :, 0]) * (GDN_HEAD_DIM ** -0.5)
    k = _l2norm(qkv[:, :, 1])
    v = qkv[:, :, 2]
    beta = jax.nn.sigmoid(beta_b.astype(f32))
    g = -jnp.exp(gdn_a_log.astype(f32)) * jax.nn.softplus(a_b.astype(f32) + gdn_dt_bias.astype(f32))

    def head_major(t):
        return jnp.moveaxis(pad(t), 2, 1)

    ob, ssm_b_new = _gdn_scan(head_major(q), head_major(k), head_major(v), head_major(g),
                              head_major(beta), ssm_b.astype(f32))
    ob = jnp.moveaxis(ob, 1, 2)[:, lpad:lpad + l]
    ob = _rmsnorm(ob, gdn_norm_w) * jax.nn.silu(z_b.astype(f32).reshape(b, l, GDN_HEADS, GDN_HEAD_DIM))
    ob = ob.reshape(b, l, GDN_WIDTH)

    xd = x.dtype
    mix = (jax.nn.sigmoid(gate_a) * (ya.astype(xd) @ w_out[:SSD_D_INNER])
           + jax.nn.sigmoid(gate_b) * (ob.astype(xd) @ w_out[SSD_D_INNER:]))
    h = x + mix

    hid = jnp.square(jax.nn.relu(_rmsnorm(h, norm_mlp_w) @ w_up))
    h = h + hid @ w_down
    return (h, conv_a_new,
            ssm_a_new.reshape(b, SSD_HEADS, SSD_HEAD_DIM, SSD_STATE).astype(ssm_a.dtype),
            conv_b_new, ssm_b_new.astype(ssm_b.dtype))


def setup_inputs(seed: int = 0) -> dict:
    key = jax.random.key(seed)
    ks = iter(jax.random.split(key, 40))
    L = DEPTH

    def nrm(shape, s):
        return jax.random.normal(next(ks), shape, jnp.float32) * s

    def gain(shape):
        return 1.0 + nrm(shape, 0.01)

    def dt_bias(n):
        dt = jnp.exp(jax.random.uniform(next(ks), (L, n), jnp.float32, math.log(1e-3), math.log(1e-1)))
        return dt + jnp.log(-jnp.expm1(-dt))

    def a_log(n):
        return jnp.log(jax.random.uniform(next(ks), (L, n), jnp.float32, 1.0, 16.0))

    return {
        "x_prompt": nrm((BATCH, SEQ, D_MODEL), 1.0),
        "x_sample": nrm((DEC_BATCH, DEC_SEQ, D_MODEL), 1.0),
        "state_ssd_conv": nrm((L, DEC_BATCH, CONV_K - 1, SSD_CONV_CH), 1.0),
        "state_ssd": nrm((L, DEC_BATCH, SSD_HEADS, SSD_HEAD_DIM, SSD_STATE), 0.1),
        "state_gdn_conv": nrm((L, DEC_BATCH, CONV_K - 1, GDN_CONV_CH), 1.0),
        "state_gdn": nrm((L, DEC_BATCH, GDN_HEADS, GDN_HEAD_DIM, GDN_HEAD_DIM), 0.1),
        "meta_tokens": nrm((N_META, D_MODEL), 1.0),
        "norm_mix_w": gain((L, D_MODEL)),
        "w_in": nrm((L, D_MODEL, IN_COLS), D_MODEL ** -0.5),
        "ssd_conv_w": nrm((L, CONV_K, SSD_CONV_CH), CONV_K ** -0.5),
        "ssd_conv_b": nrm((L, SSD_CONV_CH), 0.01),
        "ssd_dt_bias": dt_bias(SSD_HEADS),
        "ssd_a_log": a_log(SSD_HEADS),
        "ssd_d": gain((L, SSD_HEADS)),
        "ssd_norm_w": gain((L, SSD_D_INNER)),
        "gdn_conv_w": nrm((L, CONV_K, GDN_CONV_CH), CONV_K ** -0.5),
        "gdn_dt_bias": dt_bias(GDN_HEADS),
        "gdn_a_log": a_log(GDN_HEADS),
        "gdn_norm_w": gain((L, GDN_HEAD_DIM)),
        "w_out": nrm((L, MIX_WIDTH, D_MODEL), MIX_WIDTH ** -0.5),
        "norm_mlp_w": gain((L, D_MODEL)),
        "w_up": nrm((L, D_MODEL, D_FF), D_MODEL ** -0.5),
        "w_down": nrm((L, D_FF, D_MODEL), D_FF ** -0.5),
        "norm_f_w": gain((D_MODEL,)),
    }


def reference(x_prompt, x_sample, state_ssd_conv, state_ssd, state_gdn_conv, state_gdn,
              meta_tokens, norm_mix_w, w_in, ssd_conv_w, ssd_conv_b, ssd_dt_bias, ssd_a_log, ssd_d,
              ssd_norm_w, gdn_conv_w, gdn_dt_bias, gdn_a_log, gdn_norm_w, w_out, norm_mlp_w,
              w_up, w_down, norm_f_w):
    bp = x_prompt.shape[0]
    dtp = x_prompt.dtype
    hp = jnp.concatenate([jnp.broadcast_to(meta_tokens.astype(dtp)[None], (bp, N_META, D_MODEL)), x_prompt], axis=1)
    hs = x_sample
    zc_a = jnp.zeros((bp, CONV_K - 1, SSD_CONV_CH), dtp)
    zs_a = jnp.zeros((bp, SSD_HEADS, SSD_HEAD_DIM, SSD_STATE), dtp)
    zc_b = jnp.zeros((bp, CONV_K - 1, GDN_CONV_CH), dtp)
    zs_b = jnp.zeros((bp, GDN_HEADS, GDN_HEAD_DIM, GDN_HEAD_DIM), dtp)
    outs_p = ([], [], [], [])
    outs_s = ([], [], [], [])
    for i in range(DEPTH):
        lw = (norm_mix_w[i], w_in[i], ssd_conv_w[i], ssd_conv_b[i], ssd_dt_bias[i], ssd_a_log[i], ssd_d[i],
              ssd_norm_w[i], gdn_conv_w[i], gdn_dt_bias[i], gdn_a_log[i], gdn_norm_w[i], w_out[i],
              norm_mlp_w[i], w_up[i], w_down[i])
        hp, *st_p = _layer(hp, META_PAD, zc_a, zs_a, zc_b, zs_b, *lw)
        hs, *st_s = _layer(hs, 0, state_ssd_conv[i], state_ssd[i], state_gdn_conv[i], state_gdn[i], *lw)
        for lst, t in zip(outs_p, st_p):
            lst.append(t)
        for lst, t in zip(outs_s, st_s):
            lst.append(t)
    y_prompt = _rmsnorm(hp, norm_f_w)[:, N_META:]
    y_sample = _rmsnorm(hs, norm_f_w)
    ssd_conv_p, ssd_p, gdn_conv_p, gdn_p = (jnp.stack(t) for t in outs_p)
    ssd_conv_s, ssd_s, gdn_conv_s, gdn_s = (jnp.stack(t) for t in outs_s)
    return (y_prompt, y_sample, ssd_conv_p, ssd_p, gdn_conv_p, gdn_p, ssd_conv_s, ssd_s, gdn_conv_s, gdn_s)
```

```python
import contextlib
import numpy as np
import concourse.bass as bass
import concourse.mybir as mybir
from concourse.bass_utils import run_bass_kernel_spmd

F32 = mybir.dt.float32
BF16 = mybir.dt.bfloat16
F32R = mybir.dt.float32r


def r32(ap):
    return ap.bitcast(F32R)
AF = mybir.ActivationFunctionType
ALU = mybir.AluOpType

D = 2048
KC = 16
NIN = 17472
OFF_ZA, OFF_XBC, OFF_DT, OFF_QKV, OFF_ZB, OFF_BETA, OFF_A, OFF_GA, OFF_GB = (
    0, 2048, 5120, 5152, 11296, 13344, 13360, 13376, 15424)
EPS = 1e-6
NCH = 2
NEG = -1.0e30
NRING = 3
import os
KSK = os.environ.get('KSK', '')


class TK:
    NDMA = 24

    def __init__(self, nc, es):
        self.nc = nc
        self.eng = {"pe": nc.tensor, "act": nc.scalar, "dve": nc.vector, "pool": nc.gpsimd, "sp": nc.sync}
        self.sem = {k: es.enter_context(nc.semaphore("s_" + k)) for k in ("pe", "act", "dve", "pool")}
        self.cnt = {k: 0 for k in self.sem}
        self.dsem = [es.enter_context(nc.semaphore("d%d" % i)) for i in range(self.NDMA)]
        self.dcnt = 0
        self.known = {k: {} for k in self.eng}
        self.lastw = {}
        self.readers = {}

    def _wait(self, stream, tok):
        sem, val = tok[0], tok[1]
        kn = self.known[stream]
        if kn.get(id(sem), 0) >= val:
            return
        self.eng[stream].wait_ge(sem, val)
        kn[id(sem)] = val

    def _deps(self, stream, reads, writes, is_dma):
        toks = []
        for k in reads:
            lw = self.lastw.get(k)
            if lw is not None:
                toks.append(lw)
        for k in writes:
            lw = self.lastw.get(k)
            if lw is not None:
                toks.append(lw)
            toks.extend(self.readers.get(k, ()))
        for t in toks:
            if (not is_dma) and t[2] == stream and stream == "pe":
                continue
            self._wait(stream, t)

    def _commit(self, tok, reads, writes):
        for k in reads:
            lst = self.readers.setdefault(k, [])
            if tok[2] != "dma":
                lst[:] = [r for r in lst if r[2] != tok[2]]
            lst.append(tok)
        for k in writes:
            self.lastw[k] = tok
            self.readers[k] = []

    def op(self, stream, reads, writes, fn):
        bk = [k for k in reads if isinstance(k, str) and len(k) == 2 and k[0] == "b" and k[1].isdigit()]
        if bk:
            reads = [k for k in reads if k not in bk]
            writes = list(writes) + bk
        self._deps(stream, reads, writes, False)
        ins = fn()
        self.cnt[stream] += 1
        ins.then_inc(self.sem[stream], 1)
        tok = (self.sem[stream], self.cnt[stream], stream)
        self._commit(tok, reads, writes)
        return tok

    def dma(self, out, in_, reads, writes, stream="sp"):
        j = self.dcnt
        self.dcnt += 1
        sem = self.dsem[j % self.NDMA]
        prev = 16 * (j // self.NDMA)
        if prev > 0:
            self._wait(stream, (sem, prev))
        self._deps(stream, reads, writes, True)
        self.eng[stream].dma_start(out=out, in_=in_).then_inc(sem, 16)
        tok = (sem, prev + 16, "dma")
        self._commit(tok, reads, writes)
        return tok

    def finish(self):
        for k in ("pe", "act", "dve", "pool"):
            if self.cnt[k] > 0:
                self._wait("sp", (self.sem[k], self.cnt[k]))
        for i, sem in enumerate(self.dsem):
            n = (self.dcnt - i + self.NDMA - 1) // self.NDMA if self.dcnt > i else 0
            if n > 0:
                self._wait("sp", (sem, 16 * n))


def bc(ap, shape):
    return ap.broadcast_to(list(shape))


class _Stop(Exception):
    pass


def build_program(stop=None):
    nc = bass.Bass("TRN2", target_bir_lowering=False)

    def checkpoint(name):
        if stop == name:
            raise _Stop()

    def din(name, shape):
        return nc.dram_tensor(name, list(shape), F32, kind="ExternalInput").ap()

    def dout(name, shape):
        return nc.dram_tensor(name, list(shape), F32, kind="ExternalOutput").ap()

    xp = din("xp", [2, 2048, D]); xs = din("xs", [4, 64, D]); meta = din("meta", [16, D])
    sconvA = din("sconvA", [4, 3, 3072]); sA = din("sA", [4, 2048, 128])
    sconvB = din("sconvB", [4, 3, 6144]); sB = din("sB", [4, 2048, 128])
    w_in = din("w_in", [D, NIN]); w_out = din("w_out", [4096, D]); w_up = din("w_up", [D, 8192]); w_down = din("w_down", [8192, D])
    p_normmix = din("p_normmix", [1, D]); p_convwA = din("p_convwA", [4, 3072]); p_convbA = din("p_convbA", [1, 3072])
    p_dtbA = din("p_dtbA", [32, 1]); p_alogA = din("p_alogA", [1, 32]); p_dA = din("p_dA", [1, 32]); p_normA = din("p_normA", [1, D])
    p_convwB = din("p_convwB", [4, 6144]); p_dtbB = din("p_dtbB", [16, 1]); p_alogB = din("p_alogB", [16, 1]); p_normB = din("p_normB", [128, 1])
    p_normmlp = din("p_normmlp", [1, D]); p_normf = din("p_normf", [1, D])

    yp = dout("yp", [2, 2048, D]); ys = dout("ys", [4, 64, D])
    o_convA_p = dout("o_convA_p", [2, 3, 3072]); o_sA_p = dout("o_sA_p", [2, 2048, 128])
    o_convB_p = dout("o_convB_p", [2, 3, 6144]); o_sB_p = dout("o_sB_p", [2, 2048, 128])
    o_convA_s = dout("o_convA_s", [4, 3, 3072]); o_sA_s = dout("o_sA_s", [4, 2048, 128])
    o_convB_s = dout("o_convB_s", [4, 3, 6144]); o_sB_s = dout("o_sB_s", [4, 2048, 128])

    NCHUNK = 148
    wsc = nc.dram_tensor("wsc", [NCHUNK, 128, 4096], BF16, kind="Internal").ap()

    TMAX = 2 * NCH * 64
    NBLK = TMAX // 128

    with contextlib.ExitStack() as es:
        tk = TK(nc, es)

        def sb(name, shape, dt=F32):
            return es.enter_context(nc.sbuf_tensor(name, list(shape), dt))

        PSA = es.enter_context(nc.psum_tensor("PSA", [128, 4096], F32))

        def bank(i):
            return PSA[:, 512 * i:512 * (i + 1)]
        pTb = PSA[:, 0:1024].bitcast(BF16)

        R1W = TMAX * 16 + 64
        R1 = sb("R1", [128, R1W])
        R2 = sb("R2", [128, 12 * TMAX + 64])
        POSTS = [sb("post0", [128, 6 * TMAX]), sb("post1", [128, 6 * TMAX])]
        SQ = sb("SQ", [128, 4 * TMAX])
        uT = sb("uT", [128, KC, TMAX], BF16)
        yaT = sb("yaT", [128, KC, TMAX], BF16)
        obT = sb("obT", [128, KC, TMAX], BF16)
        ring = [sb("ring%d" % i, [128, 4096], BF16) for i in range(NRING)]
        hTs = [sb("hT%d" % q, [128, 2048]) for q in range(2)]
        Ss = [sb("S%d" % q, [128, 16, 128]) for q in range(2)]
        EP = sb("EP", [128, max(TMAX * 9 + 16, 4096)])
        CTR_ = [sb("CTR%d" % q, [128, 2816]) for q in range(2)]
        CTN_ = [sb("CTN%d" % q, [128, 2048]) for q in range(2)]
        SMT = sb("SMT", [128, 2, 256])
        SM128 = sb("SM128", [128, 2, 48])
        SMG = sb("SMG", [128, 2, 96]); SMG128 = sb("SMG128", [128, 2, 16])
        smF = sb("smF", [32, 3, TMAX])
        xn = sb("xn", [128, D], BF16)
        stat = sb("stat", [128, 8])
        wsm = sb("wsm", [128, KC, 64], BF16)
        identF = sb("identF", [128, 128]); identB = sb("identB", [128, 128], BF16)
        onesF = sb("onesF", [128, 128]); onesT = sb("onesT", [128, 128])
        triu = sb("triu", [128, 128]); maskU = sb("maskU", [128, 128]); maskL = sb("maskL", [128, 128]); mask01s = sb("mask01s", [128, 128])
        normmix_fm = sb("normmix_fm", [128, 16]); normmlp_fm = sb("normmlp_fm", [128, 16]); normA_fm = sb("normA_fm", [128, 16])
        convwA = sb("convwA", [128, 24, 4]); convbA = sb("convbA", [128, 24]); convwB = sb("convwB", [128, 48, 4])
        tailA = sb("tailA", [128, 24, 6]); tailB = sb("tailB", [128, 48, 6])
        a_bc = sb("a_bc", [128, 32]); D_bc = sb("D_bc", [128, 32]); D_fm = sb("D_fm", [128, 16])
        dtbA_col = sb("dtbA_col", [32, 1]); dtbB_col = sb("dtbB_col", [16, 1]); negaB_col = sb("negaB_col", [16, 1])
        normB_col = sb("normB_col", [128, 1])
        normf_bc = sb("normf_bc", [128, D])
        dummy = sb("dummyt", [128, 2])

        def V(r, w, f): return tk.op("dve", r, w, f)
        def A(r, w, f): return tk.op("act", r, w, f)
        def G(r, w, f): return tk.op("pool", r, w, f)
        def PE(r, w, f): return tk.op("pe", r, w, f)
        def BAR(r, w): return tk.op("pool", [], list(r) + list(w) + ["dummy"], lambda: nc.gpsimd.memset(dummy[:], 0.0))

        G([], ["identF"], lambda: nc.gpsimd.memset(identF[:], 0.0))
        G(["identF"], ["identF"], lambda: nc.gpsimd.affine_select(out=identF[:], in_=identF[:], compare_op=ALU.not_equal, fill=1.0, base=0, pattern=[[-1, 128]], channel_multiplier=1))
        G(["identF"], ["identB"], lambda: nc.gpsimd.tensor_copy(out=identB[:], in_=identF[:]))
        G([], ["onesT"], lambda: nc.gpsimd.memset(onesT[:], 1.0))
        V(["onesT"], ["onesF"], lambda: nc.vector.tensor_copy(out=r32(onesF[:]), in_=onesT[:]))
        G([], ["triu"], lambda: nc.gpsimd.memset(triu[:], 1.0))
        G(["triu"], ["triu"], lambda: nc.gpsimd.affine_select(out=triu[:], in_=triu[:], compare_op=ALU.is_ge, fill=0.0, base=0, pattern=[[1, 128]], channel_multiplier=-1))
        G([], ["maskU"], lambda: nc.gpsimd.memset(maskU[:], 0.0))
        G(["maskU"], ["maskU"], lambda: nc.gpsimd.affine_select(out=maskU[:], in_=maskU[:], compare_op=ALU.is_ge, fill=NEG, base=0, pattern=[[1, 128]], channel_multiplier=-1))
        G([], ["maskL"], lambda: nc.gpsimd.memset(maskL[:], 0.0))
        G(["maskL"], ["maskL"], lambda: nc.gpsimd.affine_select(out=maskL[:], in_=maskL[:], compare_op=ALU.is_ge, fill=NEG, base=0, pattern=[[-1, 128]], channel_multiplier=1))
        G([], ["mask01s"], lambda: nc.gpsimd.memset(mask01s[:], 1.0))
        G(["mask01s"], ["mask01s"], lambda: nc.gpsimd.affine_select(out=mask01s[:], in_=mask01s[:], compare_op=ALU.is_gt, fill=0.0, base=0, pattern=[[-1, 128]], channel_multiplier=1))
        BAR([], ["dummy"])

        R1_KEYS = ["xt", "pre", "post", "zS", "stg"]
        EP_ALL = [("ySB", 0), ("ySB", 1), "sqE", "rnE", "sa", "sb", "EPst"]

        ALL_BANK_KEYS = ["b%d" % i for i in range(8)]

        def bank_barrier():
            BAR([], ALL_BANK_KEYS)

        def feat_from_rows(src, R, C, dest, dkey):
            for p0 in range(0, C, 2048):
                pc = min(2048, C - p0)
                n = pc // 128
                c0 = p0 // 128
                tk.dma(R1[0:R, 0:pc], src[:, p0:p0 + pc], [], R1_KEYS)
                def f():
                    r = None
                    for i in range(n):
                        r = nc.tensor.transpose(bank(2)[:, i * R:(i + 1) * R], R1[0:R, i * 128:(i + 1) * 128], identF[0:R, 0:R])
                    return r
                PE(R1_KEYS + ["identF"], ["b2"], f)
                V(["b2"], [dkey], lambda: nc.vector.tensor_copy(out=dest[:, c0:c0 + n, :], in_=bank(2)[:, 0:n * R].rearrange("p (c r) -> p c r", r=R)))

        def rows_from_feat(srcT, skey, R, C, dst):
            for p0 in range(0, C, 2048):
                pc = min(2048, C - p0)
                cb = p0 // 128
                for c0 in range(0, pc // 128, 4):
                    def f():
                        r = None
                        for i in range(4):
                            r = nc.tensor.transpose(bank(2)[0:R, i * 128:(i + 1) * 128], srcT[:, cb + c0 + i, :], identF[:, :])
                        return r
                    PE([skey, "identF"], ["b2"], f)
                    V(["b2"], R1_KEYS, lambda: nc.vector.tensor_copy(out=R1[0:R, c0 * 128:(c0 + 4) * 128], in_=bank(2)[0:R, 0:512]))
                tk.dma(dst[:, p0:p0 + pc], R1[0:R, 0:pc], R1_KEYS, [])

        feat_from_rows(p_normmix, 1, D, normmix_fm[:, :].unsqueeze(2), "normmix_fm")
        feat_from_rows(p_normmlp, 1, D, normmlp_fm[:, :].unsqueeze(2), "normmlp_fm")
        feat_from_rows(p_normA, 1, D, normA_fm[:, :].unsqueeze(2), "normA_fm")
        feat_from_rows(p_convwA, 4, 3072, convwA[:, :, :], "convwA")
        feat_from_rows(p_convbA, 1, 3072, convbA[:, :].unsqueeze(2), "convbA")
        feat_from_rows(p_convwB, 4, 6144, convwB[:, :, :], "convwB")

        tk.dma(a_bc[:], bc(p_alogA[0:1, :], [128, 32]), [], ["a_bc"])
        A(["a_bc"], ["a_bc"], lambda: nc.scalar.activation(out=a_bc[:], in_=a_bc[:], func=AF.Exp))
        V(["a_bc"], ["a_bc"], lambda: nc.vector.tensor_scalar(out=a_bc[:], in0=a_bc[:], scalar1=-1.0, scalar2=None, op0=ALU.mult))
        tk.dma(D_bc[:], bc(p_dA[0:1, :], [128, 32]), [], ["D_bc"])
        Dv = D_bc[:, :].rearrange("p (c two) -> p c two", two=2)
        V(["D_bc"], ["D_fm"], lambda: nc.vector.tensor_copy(out=D_fm[0:64, :], in_=Dv[0:64, :, 0]))
        V(["D_bc"], ["D_fm"], lambda: nc.vector.tensor_copy(out=D_fm[64:128, :], in_=Dv[64:128, :, 1]))
        tk.dma(dtbA_col[:], p_dtbA[:, :], [], ["dtbA_col"])
        tk.dma(dtbB_col[:], p_dtbB[:, :], [], ["dtbB_col"])
        tk.dma(negaB_col[:], p_alogB[:, :], [], ["negaB_col"])
        A(["negaB_col"], ["negaB_col"], lambda: nc.scalar.activation(out=negaB_col[:], in_=negaB_col[:], func=AF.Exp))
        V(["negaB_col"], ["negaB_col"], lambda: nc.vector.tensor_scalar(out=negaB_col[:], in0=negaB_col[:], scalar1=-1.0, scalar2=None, op0=ALU.mult))
        tk.dma(normB_col[:], p_normB[:, :], [], ["normB_col"])
        tk.dma(normf_bc[:], bc(p_normf[0:1, :], [128, D]), [], ["normf_bc"])

        chunks = []
        for g in range(4):
            chunks.append(("b", [OFF_ZA + 512 * g, OFF_ZA + 512 * g + 128]))
            chunks.append(("b", [OFF_ZA + 512 * g + 256, OFF_ZA + 512 * g + 384]))
            chunks.append(("b", [OFF_XBC + 512 * g, OFF_XBC + 512 * g + 128]))
            chunks.append(("b", [OFF_XBC + 512 * g + 256, OFF_XBC + 512 * g + 384]))
            chunks.append(("b", [OFF_XBC + 2048 + 128 * g, OFF_XBC + 2560 + 128 * g]))
        for j in range(8):
            chunks.append(("b", [OFF_QKV + 256 * j, OFF_QKV + 256 * j + 128]))
            chunks.append(("b", [OFF_QKV + 2048 + 256 * j, OFF_QKV + 2048 + 256 * j + 128]))
            chunks.append(("b", [OFF_QKV + 4096 + 256 * j, OFF_QKV + 4096 + 256 * j + 128]))
            chunks.append(("b", [OFF_ZB + 256 * j, OFF_ZB + 256 * j + 128]))
        for cg in range(4):
            for off in (OFF_GA, OFF_GB):
                for kh in range(2):
                    chunks.append(("a", w_in, kh * 8, off + 512 * cg))
            for kq in range(4):
                chunks.append(("a", w_out, kq * 8, 512 * cg))
        for fb in range(4):
            for c in range(8):
                chunks.append(("bu", 2048 * fb + 256 * c))
            for cg in range(4):
                for kh in range(2):
                    chunks.append(("a", w_down, fb * 16 + kh * 8, 512 * cg))
        assert len(chunks) == NCHUNK

        BAR(R1_KEYS, ["stgF0", "stgF1", "stgB0", "stgB1"])
        stgF = [R1[:, 0:4096], EP[:, 0:4096]]
        stgB = [yaT[:, :, :].rearrange("p a b -> p (a b)")[:, 0:4096], obT[:, :, :].rearrange("p a b -> p (a b)")[:, 0:4096]]
        cast_eng = ["pool", "act", "dve"]
        for ci, ch in enumerate(chunks):
            sf = stgF[ci % 2]; sbf = stgB[ci % 2]
            kf = "stgF%d" % (ci % 2); kb = "stgB%d" % (ci % 2)
            if ch[0] == "b" and ch[1][1] == ch[1][0] + 128:
                off = ch[1][0]
                dstv = sf[:, 0:4096].rearrange("p (k c) -> p k c", c=256)
                tk.dma(dstv, w_in[:, off:off + 256].rearrange("(k p) c -> p k c", p=128), [], [kf])
                nel = 4096
            elif ch[0] == "b":
                for hlf in range(2):
                    off = ch[1][hlf]
                    dstv = sf[:, 0:4096].rearrange("p (k c) -> p k c", c=256)[:, :, hlf * 128:(hlf + 1) * 128]
                    tk.dma(dstv, w_in[:, off:off + 128].rearrange("(k p) c -> p k c", p=128), [], [kf])
                nel = 4096
            elif ch[0] == "bu":
                off = ch[1]
                dstv = sf[:, 0:4096].rearrange("p (k c) -> p k c", c=256)
                tk.dma(dstv, w_up[:, off:off + 256].rearrange("(k p) c -> p k c", p=128), [], [kf])
                nel = 4096
            else:
                Wt, r0, c0 = ch[1], ch[2], ch[3]
                dstv = sf[:, 0:4096].rearrange("p (k c) -> p k c", c=512)
                tk.dma(dstv, Wt[r0 * 128:(r0 + 8) * 128, c0:c0 + 512].rearrange("(k p) c -> p k c", p=128), [], [kf])
                nel = 4096
            e = cast_eng[ci % 3]
            if e == "pool":
                G([kf], [kb], lambda: nc.gpsimd.tensor_copy(out=sbf, in_=sf[:, 0:4096]))
            elif e == "act":
                A([kf], [kb], lambda: nc.scalar.copy(out=sbf, in_=sf[:, 0:4096]))
            else:
                V([kf], [kb], lambda: nc.vector.tensor_copy(out=sbf, in_=sf[:, 0:4096]))
            tk.dma(wsc[ci], sbf, [kb], [("wsc", ci)])
        st = R1[:, 0:1024].rearrange("p (k c) -> p k c", c=64)
        tk.dma(st[:, :, 0:32], w_in[:, OFF_DT:OFF_DT + 32].rearrange("(k p) c -> p k c", p=128), [], ["stgF0"])
        tk.dma(st[:, :, 32:64], w_in[:, OFF_BETA:OFF_BETA + 32].rearrange("(k p) c -> p k c", p=128), [], ["stgF0"])
        V(["stgF0"], ["wsm"], lambda: nc.vector.tensor_copy(out=wsm[:], in_=st))
        BAR(["stgF0", "stgF1", "stgB0", "stgB1"], R1_KEYS + EP_ALL + ["yaT", "obT"])

        wstate = {"issued": 0, "total": 0}

        def wissue(upto):
            while wstate["issued"] <= upto and wstate["issued"] < wstate["total"]:
                i = wstate["issued"]
                b = i % NRING
                tk.dma(ring[b][:, :], wsc[i % NCHUNK], [("wsc", i % NCHUNK)], [("ring", b)])
                wstate["issued"] += 1

        wcur = {"i": 0}

        def wnext():
            i = wcur["i"]
            wissue(i + NRING - 1)
            wcur["i"] += 1
            b = i % NRING
            return ring[b], ("ring", b)

        def process_tile(nch, Lc, blocks, seq_out, kind):
            W = nch * Lc
            T = 2 * W
            nslot = 2 * nch
            Lg = min(128, W)
            nchg = W // Lg
            LVG = {128: 6, 64: 5, 16: 3}[Lg]
            xt = R1[:, 0:len(blocks) * D].rearrange("p (b d) -> p b d", d=D)
            pre = R1[:, 0:6 * 2 * (W + 3)].rearrange("p (i s w) -> p i s w", i=6, s=2)
            o1 = 6 * 2 * (W + 3)
            o2 = o1 + 2 * T
            zS = R1[:, o2:o2 + 4 * T].rearrange("p (i t) -> p i t", i=4)
            assert o2 + 4 * T <= R1W
            ySB = EP[:, 0:4 * T].rearrange("p (i t) -> p i t", i=4)
            sqE = SQ[:, 0:4 * T].rearrange("p (i t) -> p i t", i=4)
            rnE = EP[:, 4 * T:5 * T]

            def load_x():
                for b, (col0, srcs, _o) in enumerate(blocks):
                    for (src, p0, n) in srcs:
                        tk.dma(xt[p0:p0 + n, b, :], src, [], R1_KEYS)

            def norm_to_T(wfm, wkey):
                for b, (col0, srcs, _o) in enumerate(blocks):
                    n = sum(s[2] for s in srcs)
                    A(["xt"], ["xn", "stat"], lambda: nc.scalar.activation(out=xn[0:n, :], in_=xt[0:n, b, :], func=AF.Square, accum_out=stat[0:n, 0:1]))
                    A(["stat"], ["stat"], lambda: nc.scalar.activation(out=stat[0:n, 1:2], in_=stat[0:n, 0:1], func=AF.Sqrt, scale=1.0 / D, bias=EPS))
                    V(["stat"], ["stat"], lambda: nc.vector.reciprocal(out=stat[0:n, 2:3], in_=stat[0:n, 1:2]))
                    A(["xt", "stat"], ["xn"], lambda: nc.scalar.activation(out=xn[0:n, :], in_=xt[0:n, b, :], func=AF.Copy, scale=stat[0:n, 2:3]))
                    def f():
                        r = None
                        for kc in range(KC):
                            r = nc.tensor.transpose(pTb[:, kc * 128:kc * 128 + n], xn[0:n, kc * 128:(kc + 1) * 128], identB[0:n, 0:n])
                        return r
                    PE(["xn", "identB"], ["b0", "b1"], f)
                    V(["b0", "b1", wkey], ["uT"], lambda: nc.vector.tensor_tensor(
                        out=uT[:, :, col0:col0 + n], in0=pTb[:, :].rearrange("p (k c) -> p k c", c=128)[:, :, 0:n],
                        in1=bc(wfm[:, :].unsqueeze(2), [128, KC, n]), op=ALU.mult))

            load_x()
            norm_to_T(normmix_fm, "normmix_fm")
            BAR(["xt"], ["pre", "post", "zS"])

            checkpoint("phase0")
            def fsm():
                r = None
                for (bk, c0, m) in ((2, 0, 32), (3, 32, 16), (4, 48, 16)):
                    for kc in range(KC):
                        r = nc.tensor.matmul(bank(bk)[0:m, 0:T], lhsT=wsm[:, kc, c0:c0 + m], rhs=uT[:, kc, 0:T], start=(kc == 0), stop=(kc == KC - 1))
                return r
            PE(["uT", "wsm"], ["b2", "b3", "b4"], fsm)
            A(["b2", "dtbA_col"], ["smF"], lambda: nc.scalar.activation(out=smF[0:32, 0, 0:T], in_=bank(2)[0:32, 0:T], func=AF.Exp, bias=dtbA_col[:, 0:1]))
            A(["smF"], ["smF"], lambda: nc.scalar.activation(out=smF[0:32, 0, 0:T], in_=smF[0:32, 0, 0:T], func=AF.Ln, bias=1.0))
            A(["b3"], ["smF"], lambda: nc.scalar.activation(out=smF[0:16, 1, 0:T], in_=bank(3)[0:16, 0:T], func=AF.Sigmoid))
            A(["b4", "dtbB_col"], ["smF"], lambda: nc.scalar.activation(out=smF[0:16, 2, 0:T], in_=bank(4)[0:16, 0:T], func=AF.Exp, bias=dtbB_col[:, 0:1]))
            A(["smF"], ["smF"], lambda: nc.scalar.activation(out=smF[0:16, 2, 0:T], in_=smF[0:16, 2, 0:T], func=AF.Ln, bias=1.0))
            V(["smF", "negaB_col"], ["smF"], lambda: nc.vector.tensor_scalar(out=smF[0:16, 2, 0:T], in0=smF[0:16, 2, 0:T], scalar1=negaB_col[:, 0:1], scalar2=None, op0=ALU.mult))
            for q in range(2):
                for c in range(nch):
                    sl = q * nch + c
                    cols = slice(q * W + c * Lc, q * W + (c + 1) * Lc)
                    sk = ("smt", sl)
                    def f():
                        nc.tensor.transpose(bank(2)[0:Lc, 0:32], smF[0:32, 0, cols], identF[0:32, 0:32])
                        nc.tensor.transpose(bank(2)[0:Lc, 32:48], smF[0:16, 1, cols], identF[0:16, 0:16])
                        return nc.tensor.transpose(bank(2)[0:Lc, 48:64], smF[0:16, 2, cols], identF[0:16, 0:16])
                    PE(["smF", "identF"], ["b2"], f)
                    V(["b2"], [sk], lambda: nc.vector.tensor_copy(out=SMT[0:Lc, sl, 0:32], in_=bank(2)[0:Lc, 0:32]))
                    V(["b2"], [sk], lambda: nc.vector.tensor_copy(out=SMT[0:Lc, sl, 160:192], in_=bank(2)[0:Lc, 32:64]))
                    V([sk, "a_bc"], [sk], lambda: nc.vector.tensor_tensor(out=SMT[0:Lc, sl, 32:64], in0=SMT[0:Lc, sl, 0:32], in1=a_bc[0:Lc, :], op=ALU.mult))
                    def f2():
                        nc.tensor.matmul(bank(3)[0:Lc, 0:32], lhsT=triu[0:Lc, 0:Lc], rhs=SMT[0:Lc, sl, 32:64], start=True, stop=True)
                        nc.tensor.matmul(bank(3)[0:Lc, 32:48], lhsT=triu[0:Lc, 0:Lc], rhs=SMT[0:Lc, sl, 176:192], start=True, stop=True)
                        nc.tensor.matmul(bank(3)[:, 64:96], lhsT=onesF[0:Lc, :], rhs=SMT[0:Lc, sl, 32:64], start=True, stop=True)
                        return nc.tensor.matmul(bank(3)[:, 96:112], lhsT=onesF[0:Lc, :], rhs=SMT[0:Lc, sl, 176:192], start=True, stop=True)
                    PE([sk, "triu", "onesF"], ["b3"], f2)
                    V(["b3"], [sk], lambda: nc.vector.tensor_copy(out=SMT[0:Lc, sl, 64:96], in_=bank(3)[0:Lc, 0:32]))
                    V(["b3"], [sk], lambda: nc.vector.tensor_copy(out=SMT[0:Lc, sl, 192:208], in_=bank(3)[0:Lc, 32:48]))
                    A(["b3"], [("sm128", sl)], lambda: nc.scalar.activation(out=SM128[:, sl, 0:48], in_=bank(3)[:, 64:112], func=AF.Exp))
                    V(["b3", sk], [sk], lambda: nc.vector.tensor_tensor(out=SMT[0:Lc, sl, 96:128], in0=bank(3)[0:Lc, 64:96], in1=SMT[0:Lc, sl, 64:96], op=ALU.subtract))
                    V(["b3", sk], [sk], lambda: nc.vector.tensor_tensor(out=SMT[0:Lc, sl, 208:224], in0=bank(3)[0:Lc, 96:112], in1=SMT[0:Lc, sl, 192:208], op=ALU.subtract))
                    A([sk], [sk], lambda: nc.scalar.activation(out=SMT[0:Lc, sl, 96:128], in_=SMT[0:Lc, sl, 96:128], func=AF.Exp))
                    A([sk], [sk], lambda: nc.scalar.activation(out=SMT[0:Lc, sl, 208:224], in_=SMT[0:Lc, sl, 208:224], func=AF.Exp))
                    A([sk], [sk], lambda: nc.scalar.activation(out=SMT[0:Lc, sl, 128:160], in_=SMT[0:Lc, sl, 64:96], func=AF.Exp))
                    A([sk], [sk], lambda: nc.scalar.activation(out=SMT[0:Lc, sl, 224:240], in_=SMT[0:Lc, sl, 192:208], func=AF.Exp))
                    V([sk], [sk], lambda: nc.vector.tensor_tensor(out=SMT[0:Lc, sl, 224:240], in0=SMT[0:Lc, sl, 224:240], in1=SMT[0:Lc, sl, 160:176], op=ALU.mult))
                    V([sk], [sk], lambda: nc.vector.tensor_scalar(out=SMT[0:Lc, sl, 240:256], in0=SMT[0:Lc, sl, 160:176], scalar1=-1.0, scalar2=None, op0=ALU.mult))

            for q in range(2):
                for c in range(nchg):
                    sl = q * nchg + c
                    cols = slice(q * W + c * Lg, q * W + (c + 1) * Lg)
                    sk = ("smg", sl)
                    def f():
                        nc.tensor.transpose(bank(2)[0:Lg, 0:16], smF[0:16, 1, cols], identF[0:16, 0:16])
                        return nc.tensor.transpose(bank(2)[0:Lg, 16:32], smF[0:16, 2, cols], identF[0:16, 0:16])
                    PE(["smF", "identF"], ["b2"], f)
                    V(["b2"], [sk], lambda: nc.vector.tensor_copy(out=SMG[0:Lg, sl, 0:32], in_=bank(2)[0:Lg, 0:32]))
                    def f2():
                        nc.tensor.matmul(bank(3)[0:Lg, 0:16], lhsT=triu[0:Lg, 0:Lg], rhs=SMG[0:Lg, sl, 16:32], start=True, stop=True)
                        return nc.tensor.matmul(bank(3)[:, 32:48], lhsT=onesF[0:Lg, :], rhs=SMG[0:Lg, sl, 16:32], start=True, stop=True)
                    PE([sk, "triu", "onesF"], ["b3"], f2)
                    V(["b3"], [sk], lambda: nc.vector.tensor_copy(out=SMG[0:Lg, sl, 32:48], in_=bank(3)[0:Lg, 0:16]))
                    A(["b3"], [("smg128", sl)], lambda: nc.scalar.activation(out=SMG128[:, sl, 0:16], in_=bank(3)[:, 32:48], func=AF.Exp))
                    V(["b3", sk], [sk], lambda: nc.vector.tensor_tensor(out=SMG[0:Lg, sl, 48:64], in0=bank(3)[0:Lg, 32:48], in1=SMG[0:Lg, sl, 32:48], op=ALU.subtract))
                    A([sk], [sk], lambda: nc.scalar.activation(out=SMG[0:Lg, sl, 48:64], in_=SMG[0:Lg, sl, 48:64], func=AF.Exp))
                    A([sk], [sk], lambda: nc.scalar.activation(out=SMG[0:Lg, sl, 64:80], in_=SMG[0:Lg, sl, 32:48], func=AF.Exp))
                    V([sk], [sk], lambda: nc.vector.tensor_tensor(out=SMG[0:Lg, sl, 64:80], in0=SMG[0:Lg, sl, 64:80], in1=SMG[0:Lg, sl, 0:16], op=ALU.mult))
                    V([sk], [sk], lambda: nc.vector.tensor_scalar(out=SMG[0:Lg, sl, 80:96], in0=SMG[0:Lg, sl, 0:16], scalar1=-1.0, scalar2=None, op0=ALU.mult))
            checkpoint("small")

            pm = {"i": 0}

            def proj_b(wbuf, wkey, half, banks=(0, 1, 2, 3, 4, 5, 6, 7)):
                bk = banks[pm["i"] % len(banks)]
                pm["i"] += 1
                def f():
                    r = None
                    for kc in range(KC):
                        r = nc.tensor.matmul(bank(bk)[:, 0:T], lhsT=wbuf[:, kc * 256 + half * 128: kc * 256 + (half + 1) * 128], rhs=uT[:, kc, 0:T], start=(kc == 0), stop=(kc == KC - 1))
                    return r
                PE(["uT", wkey], ["b%d" % bk], f)
                return bk

            def make_set(bs, Rf):
                pre = Rf[:, 0:6 * 2 * (W + 3)].rearrange("p (i s w) -> p i s w", i=6, s=2)
                post = POSTS[bs][:, 0:6 * T].rearrange("p (i t) -> p i t", i=6)
                zS = Rf[:, o2:o2 + 4 * T].rearrange("p (i t) -> p i t", i=4)
                PRE, POST, ZS = ("pre", "post", "zS") if bs == 0 else ("pre1", "post1", "zS1")
                def conv_tile(i, cw, cb, ct, tail, tkey):
                    pk = ("pre", bs, i); ok = ("post", bs, i); ck = ("cacc", bs, i % 2)
                    cflat = Rf[:, o1 + (i % 2) * T:o1 + (i % 2 + 1) * T]
                    pv = cflat.rearrange("p (s w) -> p s w", s=2)
                    if cb is not None:
                        V([PRE, pk], [ck], lambda: nc.vector.tensor_scalar(out=pv, in0=pre[:, i, :, 0:W], scalar1=cw[:, ct, 0:1], scalar2=cb[:, ct:ct + 1], op0=ALU.mult, op1=ALU.add))
                    else:
                        V([PRE, pk], [ck], lambda: nc.vector.tensor_scalar(out=pv, in0=pre[:, i, :, 0:W], scalar1=cw[:, ct, 0:1], scalar2=None, op0=ALU.mult))
                    for k in range(1, 4):
                        V([PRE, pk, ck], [ck], lambda: nc.vector.scalar_tensor_tensor(out=pv, in0=pre[:, i, :, k:k + W], scalar=cw[:, ct, k:k + 1], in1=pv, op0=ALU.mult, op1=ALU.add))
                    A([ck], [ok], lambda: nc.scalar.activation(out=r32(post[:, i, :]), in_=cflat, func=AF.Silu))
                    G([PRE, pk], [tkey], lambda: nc.gpsimd.tensor_copy(out=tail[:, ct, :].rearrange("p (s r) -> p s r", s=2), in_=pre[:, i, :, W:W + 3]))

                def load_pre(i, bk, ct, tail, tkey):
                    pk = ("pre", bs, i)
                    G([tkey, PRE], [pk], lambda: nc.gpsimd.tensor_copy(out=pre[:, i, :, 0:3], in_=tail[:, ct, :].rearrange("p (s r) -> p s r", s=2)))
                    A(["b%d" % bk, PRE], [pk], lambda: nc.scalar.copy(out=pre[:, i, :, 3:3 + W], in_=bank(bk)[:, 0:T].rearrange("p (s w) -> p s w", s=2)))

                def runpair(gens):
                    if 'n' in KSK:
                        return
                    live = list(gens)
                    while live:
                        for gq in list(live):
                            try:
                                next(gq)
                            except StopIteration:
                                live.remove(gq)

                def ssd_chunk(g, q, c):
                    sl = q * nch + c
                    cols = slice(q * W + c * Lc, q * W + (c + 1) * Lc)
                    sk = ("smt", sl)
                    ia, ib, ic = ((2, 3, 4), (7, 6, 5))[q]
                    Aq, Bq, Cq = bank(ia), bank(ib), bank(ic)
                    ka, kb_, kc_ = "b%d" % ia, "b%d" % ib, "b%d" % ic
                    TR = CTR_[q]; TN = CTN_[q]
                    tkq = lambda nm: ("ct", q, nm)
                    H4 = 4 * Lc
                    xdt = TR[0:Lc, 0:512].rearrange("p (h x) -> p h x", h=8)
                    Btok = TR[0:Lc, 512:640]
                    MTs = [TR[0:Lc, 640 + hh * 512:640 + hh * 512 + H4].rearrange("p (h t) -> p h t", h=4) for hh in range(2)]
                    xdtw = TR[0:Lc, 1664:2176]
                    ATf = TN[0:Lc, 0:H4]
                    AT = ATf.rearrange("p (h t) -> p h t", h=4)
                    LTf = TN[0:Lc, 512:512 + H4]
                    LT = LTf.rearrange("p (h t) -> p h t", h=4)
                    yv = TN[0:Lc, 1024:1536]
                    hs = TN[:, 1536:2048]
                    hst = hTs[q][:, g * 512:(g + 1) * 512]
                    hk = ("hT", q, g)
                    def f():
                        r = None
                        for i in range(4):
                            r = nc.tensor.transpose(Aq[0:Lc, i * 128:(i + 1) * 128], post[:, i, cols], identF[:, :])
                        r = nc.tensor.transpose(Bq[0:Lc, 0:128], post[:, 4, cols], identF[:, :])
                        r = nc.tensor.matmul(Bq[0:Lc, 128:128 + Lc], lhsT=r32(post[:, 4, cols]), rhs=r32(post[:, 5, cols]), start=True, stop=True)
                        return r
                    PE([("post", bs, i) for i in range(6)] + ["identF"], [ka, kb_], f)
                    V([ka, sk], [tkq("xdt")], lambda: nc.vector.tensor_tensor(out=r32(xdt), in0=Aq[0:Lc, 0:512].rearrange("p (h x) -> p h x", h=8), in1=bc(SMT[0:Lc, sl, 8 * g:8 * g + 8].unsqueeze(2), [Lc, 8, 64]), op=ALU.mult))
                    A([kb_], [tkq("Btok")], lambda: nc.scalar.copy(out=r32(Btok), in_=Bq[0:Lc, 0:128]))
                    A([kb_], [tkq("cbT"), tkq("hs")], lambda: nc.scalar.copy(out=TN[0:Lc, 1536:1536 + Lc], in_=Bq[0:Lc, 128:128 + Lc]))
                    cbT = TN[0:Lc, 1536:1536 + Lc]
                    for hh in range(2):
                        hb = 8 * g + 4 * hh
                        G([sk, "triu"], [tkq("AT")], lambda: nc.gpsimd.tensor_tensor(out=AT, in0=bc(SMT[0:Lc, sl, 32 + hb:36 + hb].unsqueeze(2), [Lc, 4, Lc]), in1=bc(triu[0:Lc, 0:Lc].unsqueeze(1), [Lc, 4, Lc]), op=ALU.mult))
                        PE([tkq("AT"), "onesF"], [kc_], lambda: nc.tensor.matmul(Cq[0:Lc, 0:H4], lhsT=onesF[0:Lc, 0:Lc], rhs=ATf, start=True, stop=True))
                        yield
                        V([kc_, sk], [tkq("LT")], lambda: nc.vector.tensor_tensor(out=LT, in0=Cq[0:Lc, 0:H4].rearrange("p (h t) -> p h t", h=4), in1=bc(SMT[0:Lc, sl, 64 + hb:68 + hb].unsqueeze(2), [Lc, 4, Lc]), op=ALU.subtract))
                        G([tkq("LT"), "maskU"], [tkq("LT")], lambda: nc.gpsimd.tensor_tensor(out=LT, in0=LT, in1=bc(maskU[0:Lc, 0:Lc].unsqueeze(1), [Lc, 4, Lc]), op=ALU.add))
                        A([tkq("LT")], [tkq("LT")], lambda: nc.scalar.activation(out=LTf, in_=LTf, func=AF.Exp))
                        G([tkq("LT"), tkq("cbT")], [tkq("MT%d" % hh)], lambda: nc.gpsimd.tensor_tensor(out=r32(MTs[hh]), in0=LT, in1=bc(cbT.unsqueeze(1), [Lc, 4, Lc]), op=ALU.mult))
                    def f():
                        r = None
                        for h in range(8):
                            r = nc.tensor.matmul(Aq[0:Lc, h * 64:(h + 1) * 64], lhsT=r32(MTs[h // 4][:, h % 4, :]), rhs=r32(xdt[:, h, :]), start=True, stop=True)
                        r = nc.tensor.matmul(Cq[0:Lc, 0:512], lhsT=post[:, 5, cols], rhs=hst, start=True, stop=True)
                        return r
                    PE([tkq("MT0"), tkq("MT1"), tkq("xdt"), ("post", bs, 5), hk], [ka, kc_], f)
                    yield
                    V([kc_, sk], [tkq("y")], lambda: nc.vector.tensor_tensor(out=yv.rearrange("p (h x) -> p h x", h=8), in0=Cq[0:Lc, 0:512].rearrange("p (h x) -> p h x", h=8), in1=bc(SMT[0:Lc, sl, 128 + 8 * g:136 + 8 * g].unsqueeze(2), [Lc, 8, 64]), op=ALU.mult))
                    V([ka, tkq("y")], [tkq("y")], lambda: nc.vector.tensor_tensor(out=yv, in0=yv, in1=Aq[0:Lc, 0:512], op=ALU.add))
                    G([tkq("xdt"), sk], [tkq("xdtw")], lambda: nc.gpsimd.tensor_tensor(out=r32(xdtw.rearrange("p (h x) -> p h x", h=8)), in0=xdt, in1=bc(SMT[0:Lc, sl, 96 + 8 * g:104 + 8 * g].unsqueeze(2), [Lc, 8, 64]), op=ALU.mult))
                    def f():
                        r = None
                        for i in range(4):
                            r = nc.tensor.transpose(Cq[:, i * Lc:(i + 1) * Lc], yv[:, i * 128:(i + 1) * 128], identF[0:Lc, 0:Lc])
                        r = nc.tensor.matmul(Aq[:, 0:512], lhsT=r32(Btok), rhs=r32(xdtw), start=True, stop=True)
                        return r
                    PE([tkq("y"), "identF", tkq("Btok"), tkq("xdtw")], [kc_, ka], f)
                    yield
                    A([kc_], [("ySB", q)], lambda: nc.scalar.copy(out=ySB[:, :, cols], in_=Cq[:, 0:4 * Lc].rearrange("p (i t) -> p i t", i=4)))
                    G([hk, ("sm128", sl)], [tkq("hs"), tkq("cbT")], lambda: nc.gpsimd.tensor_tensor(out=hs.rearrange("p (h x) -> p h x", h=8), in0=hst.rearrange("p (h x) -> p h x", h=8), in1=bc(SM128[:, sl, 8 * g:8 * g + 8].unsqueeze(2), [128, 8, 64]), op=ALU.mult))
                    V([tkq("hs"), ka], [hk], lambda: nc.vector.tensor_tensor(out=hst, in0=hs, in1=Aq[:, 0:512], op=ALU.add))
                    yield

                def ssd_proj(g):
                    wz0, kz0 = wnext()
                    for hlf in range(2):
                        bk = proj_b(wz0, kz0, hlf)
                        A(["b%d" % bk, ZS], [("zS", bs, hlf)], lambda: nc.scalar.activation(out=zS[:, hlf, :], in_=bank(bk)[:, 0:T], func=AF.Silu))
                        yield
                    wz1, kz1 = wnext()
                    for hlf in range(2):
                        bk = proj_b(wz1, kz1, hlf)
                        A(["b%d" % bk, ZS], [("zS", bs, 2 + hlf)], lambda: nc.scalar.activation(out=zS[:, 2 + hlf, :], in_=bank(bk)[:, 0:T], func=AF.Silu))
                        yield
                    cts = [4 * g, 4 * g + 1, 4 * g + 2, 4 * g + 3, 16 + g, 20 + g]
                    for ci2 in range(3):
                        wb, kb = wnext()
                        for hlf in range(2):
                            i = ci2 * 2 + hlf
                            bk = proj_b(wb, kb, hlf)
                            load_pre(i, bk, cts[i], tailA, "tailA")
                            conv_tile(i, convwA, convbA, cts[i], tailA, "tailA")
                            yield

                def ssd_epi(g):
                    yk = [("ySB", 0), ("ySB", 1)]
                    for i in range(4):
                        V(yk + [("post", bs, i), "D_fm"], yk, lambda: nc.vector.scalar_tensor_tensor(out=ySB[:, i, :], in0=post[:, i, :], scalar=D_fm[:, 4 * g + i:4 * g + i + 1], in1=ySB[:, i, :], op0=ALU.mult, op1=ALU.add))
                    G(yk + [("zS", bs, i) for i in range(4)], yk, lambda: nc.gpsimd.tensor_tensor(out=EP[:, 0:4 * T], in0=EP[:, 0:4 * T], in1=Rf[:, o2:o2 + 4 * T], op=ALU.mult))
                    A(yk, ["sqE"], lambda: nc.scalar.activation(out=r32(SQ[:, 0:4 * T]), in_=EP[:, 0:4 * T], func=AF.Square))
                    def f():
                        r = None
                        for i in range(4):
                            r = nc.tensor.matmul(bank(0)[:, 0:T], lhsT=r32(onesF[:, :]), rhs=r32(sqE[:, i, :]), start=(i == 0), stop=(i == 3))
                        return r
                    pm["i"] = 1
                    PE(["sqE", "onesF"], ["b0"], f)
                    A(["b0"], ["rnE"], lambda: nc.scalar.activation(out=rnE, in_=bank(0)[:, 0:T], func=AF.Sqrt, scale=1.0 / 512.0, bias=EPS))
                    V(["rnE"], ["rnE"], lambda: nc.vector.reciprocal(out=rnE, in_=rnE))
                    for i in range(4):
                        V(yk + ["rnE", "normA_fm"], ["yaT"], lambda: nc.vector.scalar_tensor_tensor(out=yaT[:, 4 * g + i, 0:T], in0=ySB[:, i, :], scalar=normA_fm[:, 4 * g + i:4 * g + i + 1], in1=rnE, op0=ALU.mult, op1=ALU.mult))

                def gdn_chunk(j, q, c):
                    h0 = 2 * j
                    sl = q * nchg + c
                    cols = slice(q * W + c * Lg, q * W + (c + 1) * Lg)
                    sk = ("smg", sl)
                    ia, ib, ic = ((2, 3, 4), (7, 6, 5))[q]
                    Aq, Bq, Cq = bank(ia), bank(ib), bank(ic)
                    ka, kb_, kc_ = "b%d" % ia, "b%d" % ib, "b%d" % ic
                    TR = CTR_[q]; TN = CTN_[q]
                    tkq = lambda nm: ("ct", q, nm)
                    L2 = 2 * Lg
                    def tR(o, n, p=Lg):
                        return TR[0:p, o:o + n].rearrange("p (h x) -> p h x", h=2)
                    def tN(o, n, p=Lg):
                        return TN[0:p, o:o + n].rearrange("p (h x) -> p h x", h=2)
                    kbg = tR(0, 256); kd = tR(256, 256); vb = tR(512, 256)
                    Pb = [tR(768, L2), tR(1024, L2)]
                    PR = [tR(1280, 4 * Lg), tR(1792, 4 * Lg)]
                    RTf = tR(2304, L2)
                    QKT = tR(2560, L2); wv = tR(1280, 256)
                    AT = tN(0, L2); Dm = tN(256, L2); egc = tN(512, L2, 128); qg = tN(768, L2, 128)
                    nbm = tN(1024, L2); QK = tN(1280, L2); nwT = tN(1536, L2, 128); ss = TN[:, 1792:2048].rearrange("p (h x) -> p h x", h=2)
                    Sst = Ss[q][:, h0:h0 + 2, :]
                    skey = ("S", q, j)
                    def f():
                        r = None
                        for h in range(2):
                            r = nc.tensor.transpose(Aq[0:Lg, h * 128:(h + 1) * 128], post[:, 2 + h, cols], identF[:, :])
                        for h in range(2):
                            r = nc.tensor.transpose(Aq[0:Lg, 256 + h * 128:256 + (h + 1) * 128], post[:, 4 + h, cols], identF[:, :])
                        return r
                    PE([("post", bs, i) for i in range(2, 6)] + ["identF"], [ka, ka], f)
                    kps = Aq[0:Lg, 0:256].rearrange("p (h x) -> p h x", h=2)
                    vps = Aq[0:Lg, 256:512].rearrange("p (h x) -> p h x", h=2)
                    V([ka, sk], [tkq("kbg")], lambda: nc.vector.tensor_tensor(out=r32(kbg), in0=kps, in1=bc(SMG[0:Lg, sl, 64 + h0:66 + h0].unsqueeze(2), [Lg, 2, 128]), op=ALU.mult))
                    V([ka, sk], [tkq("kd")], lambda: nc.vector.tensor_tensor(out=r32(kd), in0=kps, in1=bc(SMG[0:Lg, sl, 48 + h0:50 + h0].unsqueeze(2), [Lg, 2, 128]), op=ALU.mult))
                    V([ka, sk], [tkq("vb")], lambda: nc.vector.tensor_tensor(out=r32(vb), in0=vps, in1=bc(SMG[0:Lg, sl, 0 + h0:2 + h0].unsqueeze(2), [Lg, 2, 128]), op=ALU.mult))
                    G([sk, "triu"], [tkq("AT")], lambda: nc.gpsimd.tensor_tensor(out=AT, in0=bc(SMG[0:Lg, sl, 16 + h0:18 + h0].unsqueeze(2), [Lg, 2, Lg]), in1=bc(triu[0:Lg, 0:Lg].unsqueeze(1), [Lg, 2, Lg]), op=ALU.mult))
                    G([sk, "mask01s"], [tkq("nbm")], lambda: nc.gpsimd.tensor_tensor(out=nbm, in0=bc(SMG[0:Lg, sl, 80 + h0:82 + h0].unsqueeze(2), [Lg, 2, Lg]), in1=bc(mask01s[0:Lg, 0:Lg].unsqueeze(1), [Lg, 2, Lg]), op=ALU.mult))
                    def f():
                        r = nc.tensor.matmul(Bq[:, 0:L2], lhsT=onesF[0:Lg, :], rhs=TN[0:Lg, 0:L2], start=True, stop=True)
                        for h in range(2):
                            r = nc.tensor.matmul(Bq[0:Lg, 256 + h * Lg:256 + (h + 1) * Lg], lhsT=r32(post[:, 2 + h, cols]), rhs=r32(post[:, 2 + h, cols]), start=True, stop=True)
                        for h in range(2):
                            r = nc.tensor.matmul(Cq[0:Lg, h * Lg:(h + 1) * Lg], lhsT=r32(post[:, h, cols]), rhs=r32(post[:, 2 + h, cols]), start=True, stop=True)
                        return r
                    PE([tkq("AT"), "onesF"] + [("post", bs, i) for i in range(4)], [kb_, kc_], f)
                    yield
                    V([kb_, sk], [tkq("Dm")], lambda: nc.vector.tensor_tensor(out=Dm, in0=bc(SMG[0:Lg, sl, 32 + h0:34 + h0].unsqueeze(2), [Lg, 2, Lg]), in1=Bq[0:Lg, 0:L2].rearrange("p (h x) -> p h x", h=2), op=ALU.subtract))
                    G([tkq("Dm"), "maskL"], [tkq("Dm")], lambda: nc.gpsimd.tensor_tensor(out=Dm, in0=Dm, in1=bc(maskL[0:Lg, 0:Lg].unsqueeze(1), [Lg, 2, Lg]), op=ALU.add))
                    A([tkq("Dm")], [tkq("Dm")], lambda: nc.scalar.activation(out=TN[0:Lg, 256:256 + L2], in_=TN[0:Lg, 256:256 + L2], func=AF.Exp))
                    A([kb_], [tkq("egc")], lambda: nc.scalar.activation(out=TN[:, 512:512 + L2], in_=Bq[:, 0:L2], func=AF.Exp))
                    G([tkq("egc"), ("post", bs, 0), ("post", bs, 1)], [tkq("qg")], lambda: nc.gpsimd.tensor_tensor(out=qg, in0=post[:, 0:2, cols], in1=egc, op=ALU.mult))
                    V([kb_, tkq("Dm")], [tkq("P0")], lambda: nc.vector.tensor_tensor(out=r32(Pb[0]), in0=Bq[0:Lg, 256:256 + L2].rearrange("p (h x) -> p h x", h=2), in1=Dm, op=ALU.mult))
                    G([tkq("P0"), tkq("nbm")], [tkq("P0")], lambda: nc.gpsimd.tensor_tensor(out=r32(Pb[0]), in0=Pb[0], in1=nbm, op=ALU.mult))
                    V([kc_, tkq("Dm")], [tkq("QK")], lambda: nc.vector.tensor_tensor(out=QK, in0=Cq[0:Lg, 0:L2].rearrange("p (h x) -> p h x", h=2), in1=Dm, op=ALU.mult))
                    def f():
                        r = None
                        for h in range(2):
                            r = nc.tensor.transpose(Cq[0:Lg, h * Lg:(h + 1) * Lg], Pb[0][:, h, :], identF[0:Lg, 0:Lg])
                        for h in range(2):
                            r = nc.tensor.transpose(Cq[0:Lg, 256 + h * Lg:256 + (h + 1) * Lg], QK[:, h, :], identF[0:Lg, 0:Lg])
                        return r
                    PE([tkq("P0"), tkq("QK"), "identF"], [kc_], f)
                    yield
                    A([kc_], [tkq("PR0")], lambda: nc.scalar.copy(out=r32(PR[0][:, :, 0:Lg]), in_=Cq[0:Lg, 0:L2].rearrange("p (h x) -> p h x", h=2)))
                    A([kc_], [tkq("QKT")], lambda: nc.scalar.copy(out=r32(QKT), in_=Cq[0:Lg, 256:256 + L2].rearrange("p (h x) -> p h x", h=2)))
                    V(["identF"], [tkq("PR0")], lambda: nc.vector.tensor_copy(out=r32(PR[0][:, :, Lg:2 * Lg]), in_=bc(identF[0:Lg, 0:Lg].unsqueeze(1), [Lg, 2, Lg])))
                    cur = 0
                    for lv in range(LVG):
                        nxt = 1 - cur
                        last = (lv == LVG - 1)
                        P, Pn = Pb[cur], Pb[nxt]
                        PRc, PRn = PR[cur], PR[nxt]
                        kP, kPn = tkq("P%d" % cur), tkq("P%d" % nxt)
                        kPR, kPRn = tkq("PR%d" % cur), tkq("PR%d" % nxt)
                        def f():
                            r = None
                            for h in range(2):
                                r = nc.tensor.matmul(Cq[0:Lg, h * Lg:(h + 1) * Lg], lhsT=r32(PRc[:, h, 0:Lg]), rhs=r32(P[:, h, :]), start=True, stop=True)
                            for h in range(2):
                                if last:
                                    r = nc.tensor.matmul(Bq[0:Lg, h * 2 * Lg + Lg:(h + 1) * 2 * Lg], lhsT=r32(P[:, h, :]), rhs=r32(PRc[:, h, Lg:2 * Lg]), start=True, stop=True)
                                else:
                                    r = nc.tensor.matmul(Bq[0:Lg, h * 2 * Lg:(h + 1) * 2 * Lg], lhsT=r32(P[:, h, :]), rhs=r32(PRc[:, h, :]), start=True, stop=True)
                            return r
                        PE([kP, kPR], [kc_, kb_], f)
                        yield
                        A([kc_], [kPn], lambda: nc.scalar.copy(out=r32(Pn), in_=Cq[0:Lg, 0:L2].rearrange("p (h x) -> p h x", h=2)))
                        Bv = Bq[0:Lg, 0:4 * Lg].rearrange("p (h x) -> p h x", h=2)
                        if not last:
                            V([kb_], [kPRn], lambda: nc.vector.tensor_copy(out=r32(PRn[:, :, 0:Lg]), in_=Bv[:, :, 0:Lg]))
                        V([kb_, kPR], [kPRn], lambda: nc.vector.tensor_tensor(out=r32(PRn[:, :, Lg:2 * Lg]), in0=Bv[:, :, Lg:2 * Lg], in1=PRc[:, :, Lg:2 * Lg], op=ALU.add))
                        cur = nxt
                    P = Pb[cur]; PRc = PR[cur]
                    def f():
                        r = None
                        for h in range(2):
                            r = nc.tensor.matmul(Bq[0:Lg, h * Lg:(h + 1) * Lg], lhsT=r32(P[:, h, :]), rhs=r32(PRc[:, h, Lg:2 * Lg]), start=True, stop=True)
                        return r
                    PE([tkq("P%d" % cur), tkq("PR%d" % cur)], [kb_], f)
                    yield
                    RT = RTf; kRT = tkq("RTf")
                    V([kb_, tkq("PR%d" % cur)], [kRT], lambda: nc.vector.tensor_tensor(out=r32(RTf), in0=Bq[0:Lg, 0:L2].rearrange("p (h x) -> p h x", h=2), in1=PRc[:, :, Lg:2 * Lg], op=ALU.add))
                    def f():
                        r = None
                        for h in range(2):
                            r = nc.tensor.matmul(Aq[:, h * Lg:(h + 1) * Lg], lhsT=r32(kbg[:, h, :]), rhs=r32(RT[:, h, :]), start=True, stop=True)
                        return r
                    PE([tkq("kbg"), kRT], [ka], f)
                    yield
                    V([ka], [tkq("nwT")], lambda: nc.vector.tensor_scalar(out=nwT, in0=Aq[:, 0:L2].rearrange("p (h x) -> p h x", h=2), scalar1=-1.0, scalar2=None, op0=ALU.mult))
                    def f():
                        r = None
                        for h in range(2):
                            nc.tensor.matmul(Bq[0:Lg, h * 128:(h + 1) * 128], lhsT=r32(RT[:, h, :]), rhs=r32(vb[:, h, :]), start=True, stop=False)
                            r = nc.tensor.matmul(Bq[0:Lg, h * 128:(h + 1) * 128], lhsT=nwT[:, h, :], rhs=Sst[:, h, :], start=False, stop=True)
                        return r
                    PE([kRT, tkq("vb"), tkq("nwT"), skey], [kb_], f)
                    yield
                    V([kb_], [tkq("PR0")], lambda: nc.vector.tensor_copy(out=r32(wv), in_=Bq[0:Lg, 0:256].rearrange("p (h x) -> p h x", h=2)))
                    def f():
                        r = None
                        for h in range(2):
                            nc.tensor.matmul(Cq[:, 256 + h * Lg:256 + (h + 1) * Lg], lhsT=Sst[:, h, :], rhs=qg[:, h, :], start=True, stop=False)
                            r = nc.tensor.matmul(Cq[:, 256 + h * Lg:256 + (h + 1) * Lg], lhsT=r32(wv[:, h, :]), rhs=r32(QKT[:, h, :]), start=False, stop=True)
                        for h in range(2):
                            r = nc.tensor.matmul(Aq[:, 256 + h * 128:256 + (h + 1) * 128], lhsT=r32(kd[:, h, :]), rhs=r32(wv[:, h, :]), start=True, stop=True)
                        return r
                    PE([skey, tkq("qg"), tkq("PR0"), tkq("QKT"), tkq("kd")], [kc_, ka], f)
                    yield
                    A([kc_], [("ySB", q)], lambda: nc.scalar.copy(out=ySB[:, 0:2, cols], in_=Cq[:, 256:256 + L2].rearrange("p (h x) -> p h x", h=2)))
                    G([skey, ("smg128", sl)], [tkq("ss")], lambda: nc.gpsimd.tensor_tensor(out=ss, in0=Sst, in1=bc(SMG128[:, sl, h0:h0 + 2].unsqueeze(2), [128, 2, 128]), op=ALU.mult))
                    V([tkq("ss"), ka], [skey], lambda: nc.vector.tensor_tensor(out=Sst, in0=ss, in1=Aq[:, 256:512].rearrange("p (h x) -> p h x", h=2), op=ALU.add))
                    yield

                def gdn_proj(j):
                    for part in range(3):
                        wb, kb = wnext()
                        for hlf in range(2):
                            i = part * 2 + hlf
                            ct = part * 16 + 2 * j + hlf
                            bk = proj_b(wb, kb, hlf)
                            load_pre(i, bk, ct, tailB, "tailB")
                            conv_tile(i, convwB, None, ct, tailB, "tailB")
                            yield
                    wb, kb = wnext()
                    for hlf in range(2):
                        bk = proj_b(wb, kb, hlf)
                        A(["b%d" % bk, ZS], [("zS", bs, hlf)], lambda: nc.scalar.activation(out=zS[:, hlf, :], in_=bank(bk)[:, 0:T], func=AF.Silu))
                        yield
                    pk4 = [("post", bs, i) for i in range(4)]
                    A(pk4, ["sqE"], lambda: nc.scalar.activation(out=r32(SQ[:, 0:4 * T]), in_=POSTS[bs][:, 0:4 * T], func=AF.Square))
                    for i in range(4):
                        bk = pm["i"] % 2
                        pm["i"] += 1
                        PE(["sqE", "onesF"], ["b%d" % bk], lambda: nc.tensor.matmul(bank(bk)[:, 0:T], lhsT=r32(onesF[:, :]), rhs=r32(sqE[:, i, :]), start=True, stop=True))
                        A(["b%d" % bk], ["rnE"], lambda: nc.scalar.activation(out=rnE, in_=bank(bk)[:, 0:T], func=AF.Sqrt, bias=EPS))
                        V(["rnE"], ["rnE"], lambda: nc.vector.reciprocal(out=rnE, in_=rnE))
                        scl = (128.0 ** -0.5) if i < 2 else 1.0
                        V(["rnE", ("post", bs, i)], [("post", bs, i)], lambda: nc.vector.scalar_tensor_tensor(out=r32(post[:, i, :]), in0=post[:, i, :], scalar=scl, in1=rnE, op0=ALU.mult, op1=ALU.mult))
                        yield

                def gdn_epi(j):
                    ok_ = [("ySB", 0), ("ySB", 1)]
                    A(ok_, ["sqE"], lambda: nc.scalar.activation(out=r32(SQ[:, 0:2 * T]), in_=EP[:, 0:2 * T], func=AF.Square))
                    for h in range(2):
                        bk = pm["i"] % 2
                        pm["i"] += 1
                        PE(["sqE", "onesF"], ["b%d" % bk], lambda: nc.tensor.matmul(bank(bk)[:, 0:T], lhsT=r32(onesF[:, :]), rhs=r32(sqE[:, h, :]), start=True, stop=True))
                        A(["b%d" % bk], ["rnE"], lambda: nc.scalar.activation(out=rnE, in_=bank(bk)[:, 0:T], func=AF.Sqrt, scale=1.0 / 128.0, bias=EPS))
                        V(["rnE"], ["rnE"], lambda: nc.vector.reciprocal(out=rnE, in_=rnE))
                        V(ok_ + ["rnE", "normB_col"], ok_, lambda: nc.vector.scalar_tensor_tensor(out=ySB[:, h, :], in0=ySB[:, h, :], scalar=normB_col[:, 0:1], in1=rnE, op0=ALU.mult, op1=ALU.mult))
                        V(ok_ + [("zS", bs, h)], ["obT"], lambda: nc.vector.tensor_tensor(out=obT[:, 2 * j + h, 0:T], in0=ySB[:, h, :], in1=zS[:, h, :], op=ALU.mult))

                return {"ssd_proj": ssd_proj, "ssd_chunk": ssd_chunk, "ssd_epi": ssd_epi, "gdn_proj": gdn_proj, "gdn_chunk": gdn_chunk, "gdn_epi": gdn_epi}

            sets = [make_set(0, R1), make_set(1, R2)]
            phases = [("ssd", g) for g in range(4)] + [("gdn", j) for j in range(8)]

            def proj_of(k):
                kind, idx = phases[k]
                return sets[k % 2][kind + "_proj"](idx)

            def drain(gen):
                for _ in gen:
                    pass

            def run_mixed(gens, pg):
                live = list(gens)
                while live:
                    for gq in list(live):
                        try:
                            next(gq)
                        except StopIteration:
                            live.remove(gq)
                    if pg is not None:
                        try:
                            next(pg)
                        except StopIteration:
                            pg = None
                return pg

            drain(proj_of(0))
            for k in range(12):
                kind, idx = phases[k]
                S_ = sets[k % 2]
                if k + 1 < 12:
                    drain(proj_of(k + 1))
                for c in range(nch if kind == "ssd" else nchg):
                    run_mixed([S_[kind + "_chunk"](idx, 0, c), S_[kind + "_chunk"](idx, 1, c)], None)
                S_[kind + "_epi"](idx)
            checkpoint("gdn")
            nb = len(blocks)
            bn = [sum(s[2] for s in blk[1]) for blk in blocks]
            BAR(["pre", "post", "zS"] + [("pre", 0, i) for i in range(6)] + [("cacc", 0, i) for i in range(2)] + [("zS", 0, i) for i in range(4)], R1_KEYS)
            load_x()
            sa = EP[:, 0:4 * 512].rearrange("p (b c) -> p b c", c=512)
            sbv = EP[:, 2048:4096].rearrange("p (b c) -> p b c", c=512)
            EPK = [("ySB", 0), ("ySB", 1), "sqE", "rnE"]
            BAR(EPK, ["sa", "sb"])
            for cg in range(4):
                def acc_group(bank0, srcT, srckey, nchunks, kc0):
                    for chn in range(nchunks):
                        wb, kb = wnext()
                        wv_ = wb[:, :].rearrange("p (k c) -> p k c", c=512)
                        def f():
                            r = None
                            for b in range(nb):
                                col0 = blocks[b][0]
                                for kc in range(8):
                                    kk = kc0 + chn * 8 + kc
                                    r = nc.tensor.matmul(bank(bank0 + b)[0:bn[b], :], lhsT=srcT[:, kk, col0:col0 + bn[b]], rhs=wv_[:, kc, :], start=(chn == 0 and kc == 0), stop=(chn == nchunks - 1 and kc == 7))
                            return r
                        PE([srckey, kb], ["b%d" % (bank0 + b) for b in range(nb)], f)
                g0, g1, g2, g3 = (0, 2, 4, 6) if nb <= 2 else (0, 4, 0, 4)
                acc_group(g0, uT, "uT", 2, 0)
                for b in range(nb):
                    A(["b%d" % (g0 + b)], ["sa"], lambda: nc.scalar.activation(out=sa[0:bn[b], b, :], in_=bank(g0 + b)[0:bn[b], :], func=AF.Sigmoid))
                acc_group(g1, uT, "uT", 2, 0)
                for b in range(nb):
                    A(["b%d" % (g1 + b)], ["sb"], lambda: nc.scalar.activation(out=sbv[0:bn[b], b, :], in_=bank(g1 + b)[0:bn[b], :], func=AF.Sigmoid))
                acc_group(g2, yaT, "yaT", 2, 0)
                for b in range(nb):
                    V(["b%d" % (g2 + b), "sa"], ["sa"], lambda: nc.vector.tensor_tensor(out=sa[0:bn[b], b, :], in0=sa[0:bn[b], b, :], in1=bank(g2 + b)[0:bn[b], :], op=ALU.mult))
                acc_group(g3, obT, "obT", 2, 0)
                for b in range(nb):
                    V(["b%d" % (g3 + b), "sb"], ["sb"], lambda: nc.vector.tensor_tensor(out=sbv[0:bn[b], b, :], in0=sbv[0:bn[b], b, :], in1=bank(g3 + b)[0:bn[b], :], op=ALU.mult))
                for b in range(nb):
                    G(["sa", "sb"], ["sa"], lambda: nc.gpsimd.tensor_tensor(out=sa[0:bn[b], b, :], in0=sa[0:bn[b], b, :], in1=sbv[0:bn[b], b, :], op=ALU.add))
                    G(["sa", "xt"], ["xt"], lambda: nc.gpsimd.tensor_tensor(out=xt[0:bn[b], b, cg * 512:(cg + 1) * 512], in0=xt[0:bn[b], b, cg * 512:(cg + 1) * 512], in1=sa[0:bn[b], b, :], op=ALU.add))

            checkpoint("C")
            norm_to_T(normmlp_fm, "normmlp_fm")
            hid = [yaT, obT]; hkey = ["yaT", "obT"]
            rl = EP[:, 0:512]
            pm["i"] = 0
            for fb in range(4):
                hT_ = hid[fb % 2]; hk_ = hkey[fb % 2]
                for c8 in range(8):
                    wb, kb = wnext()
                    for hlf in range(2):
                        bk = proj_b(wb, kb, hlf, (0, 1, 4, 7))
                        A(["b%d" % bk, "sa"], ["sa"], lambda: nc.scalar.activation(out=rl[:, 0:T], in_=bank(bk)[:, 0:T], func=AF.Relu))
                        G(["sa"], [hk_], lambda: nc.gpsimd.tensor_tensor(out=hT_[:, c8 * 2 + hlf, 0:T], in0=rl[:, 0:T], in1=rl[:, 0:T], op=ALU.mult))
                for cg in range(4):
                    b0 = 2 + 3 * (cg % 2) if nb <= 3 else None
                    base = (2 if cg % 2 == 0 else 5) if nb <= 3 else None
                    if nb <= 3:
                        bl = [base + b for b in range(nb)]
                    else:
                        bl = [2, 3, 4, 5] if cg % 2 == 0 else [6, 7, 0, 1]
                    for kh in range(2):
                        wb, kb = wnext()
                        wv_ = wb[:, :].rearrange("p (k c) -> p k c", c=512)
                        def f():
                            r = None
                            for b in range(nb):
                                col0 = blocks[b][0]
                                for kc in range(8):
                                    r = nc.tensor.matmul(bank(bl[b])[0:bn[b], :], lhsT=hT_[:, kh * 8 + kc, col0:col0 + bn[b]], rhs=wv_[:, kc, :], start=(kh == 0 and kc == 0), stop=(kh == 1 and kc == 7))
                            return r
                        PE([hk_, kb], ["b%d" % x for x in bl[:nb]], f)
                    for b in range(nb):
                        V(["b%d" % bl[b], "xt"], ["xt"], lambda: nc.vector.tensor_tensor(out=xt[0:bn[b], b, cg * 512:(cg + 1) * 512], in0=xt[0:bn[b], b, cg * 512:(cg + 1) * 512], in1=bank(bl[b])[0:bn[b], :], op=ALU.add))
            pm["i"] = 0

            checkpoint("D")
            for b, (col0, srcs, outs) in enumerate(blocks):
                n = bn[b]
                if not outs:
                    continue
                A(["xt"], ["xn", "stat"], lambda: nc.scalar.activation(out=xn[0:n, :], in_=xt[0:n, b, :], func=AF.Square, accum_out=stat[0:n, 4:5]))
                A(["stat"], ["stat"], lambda: nc.scalar.activation(out=stat[0:n, 5:6], in_=stat[0:n, 4:5], func=AF.Sqrt, scale=1.0 / D, bias=EPS))
                V(["stat"], ["stat"], lambda: nc.vector.reciprocal(out=stat[0:n, 6:7], in_=stat[0:n, 5:6]))
                V(["xt", "stat", "normf_bc"], ["xt"], lambda: nc.vector.scalar_tensor_tensor(out=xt[0:n, b, :], in0=xt[0:n, b, :], scalar=stat[0:n, 6:7], in1=normf_bc[0:n, :], op0=ALU.mult, op1=ALU.mult))
                for (dst, p0, nn) in outs:
                    tk.dma(dst, xt[p0:p0 + nn, b, :], ["xt"], [])

        def zero_states():
            for q in range(2):
                V([], [("hT", q, g) for g in range(4)], lambda: nc.vector.memset(hTs[q][:, :], 0.0))
                G([], [("S", q, j) for j in range(8)], lambda: nc.gpsimd.memset(Ss[q][:, :, :], 0.0))
            G([], ["tailA"], lambda: nc.gpsimd.memset(tailA[:, :, :], 0.0))
            G([], ["tailB"], lambda: nc.gpsimd.memset(tailB[:, :, :], 0.0))

        def load_states(s0):
            for q in range(2):
                stg = EP[:, 0:2048].rearrange("p (c n) -> p c n", n=128)
                tk.dma(stg, sA[s0 + q].rearrange("(c r) n -> r c n", r=128), [], ["EPst"])
                for c4 in range(4):
                    def f():
                        r = None
                        for i in range(4):
                            r = nc.tensor.transpose(bank(2)[:, i * 128:(i + 1) * 128], stg[:, c4 * 4 + i, :], identF[:, :])
                        return r
                    PE(["EPst", "identF"], ["b2"], f)
                    V(["b2"], [("hT", q, c4)], lambda: nc.vector.tensor_copy(out=hTs[q][:, c4 * 512:(c4 + 1) * 512], in_=bank(2)[:, 0:512]))
                tk.dma(Ss[q][:, :, :], sB[s0 + q].rearrange("(h k) v -> k h v", k=128), [], [("S", q, j) for j in range(8)])
            feat_from_rows(sconvA[s0:s0 + 2].rearrange("s r c -> (s r) c"), 6, 3072, tailA[:, :, :], "tailA")
            feat_from_rows(sconvB[s0:s0 + 2].rearrange("s r c -> (s r) c"), 6, 6144, tailB[:, :, :], "tailB")

        def store_states(o_sA, o_sB, o_cA, o_cB, s0):
            for q in range(2):
                stg = EP[:, 0:2048].rearrange("p (c n) -> p c n", n=128)
                for c4 in range(4):
                    def f():
                        r = None
                        for i in range(4):
                            r = nc.tensor.transpose(bank(2)[:, i * 128:(i + 1) * 128], hTs[q][:, (c4 * 4 + i) * 128:(c4 * 4 + i + 1) * 128], identF[:, :])
                        return r
                    PE([("hT", q, c4), "identF"], ["b2"], f)
                    V(["b2"], ["EPst"], lambda: nc.vector.tensor_copy(out=stg[:, c4 * 4:(c4 + 1) * 4, :], in_=bank(2)[:, 0:512].rearrange("p (c n) -> p c n", n=128)))
                tk.dma(o_sA[s0 + q].rearrange("(c r) n -> r c n", r=128), stg, ["EPst"], [])
                tk.dma(o_sB[s0 + q].rearrange("(h k) v -> k h v", k=128), Ss[q][:, :, :], [("S", q, j) for j in range(8)], [])
            rows_from_feat(tailA, "tailA", 6, 3072, o_cA[s0:s0 + 2].rearrange("s r c -> (s r) c"))
            rows_from_feat(tailB, "tailB", 6, 6144, o_cB[s0:s0 + 2].rearrange("s r c -> (s r) c"))

        n_tiles = 2 + 1 + 32 // NCH
        wstate["total"] = n_tiles * NCHUNK

        def ep_barrier():
            BAR(EP_ALL, EP_ALL)

        def schedule():
            checkpoint("prologue")
            for st in range(2):
                s0 = 2 * st
                ep_barrier()
                load_states(s0)
                checkpoint("states")
                ep_barrier()
                blocks = [(0, [(xs[s0], 0, 64), (xs[s0 + 1], 64, 64)], [(ys[s0], 0, 64), (ys[s0 + 1], 64, 64)])]
                process_tile(1, 64, blocks, None, "S")
                ep_barrier()
                store_states(o_sA_s, o_sB_s, o_convA_s, o_convB_s, s0)
                checkpoint("tile0")
            ep_barrier()
            zero_states()
            blocks = [(0, [(meta[:, :], 0, 16), (meta[:, :], 16, 16)], [])]
            process_tile(1, 16, blocks, None, "M")
            if stop == "meta":
                ep_barrier()
                store_states(o_sA_p, o_sB_p, o_convA_p, o_convB_p, 0)
                checkpoint("meta")
            Wp = NCH * 64
            for ti in range(32 // NCH):
                t0 = ti * Wp
                blocks = []
                for q in range(2):
                    for bb in range(Wp // 128):
                        r0 = t0 + bb * 128
                        blocks.append((q * Wp + bb * 128, [(xp[q, r0:r0 + 128, :], 0, 128)], [(yp[q, r0:r0 + 128, :], 0, 128)]))
                process_tile(NCH * 64 // 128, 128, blocks, None, "P")
                if stop == "ptile0" or (stop == "ptile1" and ti == 1):
                    ep_barrier()
                    store_states(o_sA_p, o_sB_p, o_convA_p, o_convB_p, 0)
                    raise _Stop()
            ep_barrier()
            store_states(o_sA_p, o_sB_p, o_convA_p, o_convB_p, 0)

        try:
            schedule()
        except _Stop:
            pass
        tk.finish()
    return nc


_NC_CACHE = {}


def kernel(**inputs):
    f = lambda a: np.ascontiguousarray(np.asarray(a, dtype=np.float32))
    x_prompt = f(inputs["x_prompt"]); x_sample = f(inputs["x_sample"])
    shared = {
        "meta": f(inputs["meta_tokens"]),
        "w_in": f(inputs["w_in"][0]), "w_out": f(inputs["w_out"][0]), "w_up": f(inputs["w_up"][0]), "w_down": f(inputs["w_down"][0]),
        "p_normmix": f(inputs["norm_mix_w"]).reshape(1, D), "p_convwA": f(inputs["ssd_conv_w"][0]), "p_convbA": f(inputs["ssd_conv_b"]).reshape(1, 3072),
        "p_dtbA": f(inputs["ssd_dt_bias"]).reshape(32, 1), "p_alogA": f(inputs["ssd_a_log"]).reshape(1, 32), "p_dA": f(inputs["ssd_d"]).reshape(1, 32),
        "p_normA": f(inputs["ssd_norm_w"]).reshape(1, D), "p_convwB": f(inputs["gdn_conv_w"][0]),
        "p_dtbB": f(inputs["gdn_dt_bias"]).reshape(16, 1), "p_alogB": f(inputs["gdn_a_log"]).reshape(16, 1), "p_normB": f(inputs["gdn_norm_w"]).reshape(128, 1),
        "p_normmlp": f(inputs["norm_mlp_w"]).reshape(1, D), "p_normf": f(inputs["norm_f_w"]).reshape(1, D),
    }
    sca = f(inputs["state_ssd_conv"][0]); ssa = f(inputs["state_ssd"][0]).reshape(32, 2048, 128)
    scb = f(inputs["state_gdn_conv"][0]); ssb = f(inputs["state_gdn"][0]).reshape(32, 2048, 128)
    in_maps = []
    for c in range(8):
        m = dict(shared)
        m["xp"] = x_prompt[2 * c:2 * c + 2]
        m["xs"] = x_sample[4 * c:4 * c + 4]
        m["sconvA"] = sca[4 * c:4 * c + 4]; m["sA"] = ssa[4 * c:4 * c + 4]
        m["sconvB"] = scb[4 * c:4 * c + 4]; m["sB"] = ssb[4 * c:4 * c + 4]
        in_maps.append(m)
    if "nc" not in _NC_CACHE:
        _NC_CACHE["nc"] = build_program()
    nc = _NC_CACHE["nc"]
    res = run_bass_kernel_spmd(nc, in_maps, core_ids=list(range(8)))
    R = res.results
    cat = lambda k: np.concatenate([np.asarray(r[k], dtype=np.float32) for r in R], axis=0)
    y_prompt = cat("yp"); y_sample = cat("ys")
    outs = [y_prompt, y_sample,
            cat("o_convA_p")[None], cat("o_sA_p").reshape(1, 16, 32, 64, 128),
            cat("o_convB_p")[None], cat("o_sB_p").reshape(1, 16, 16, 128, 128),
            cat("o_convA_s")[None], cat("o_sA_s").reshape(1, 32, 32, 64, 128),
            cat("o_convB_s")[None], cat("o_sB_s").reshape(1, 32, 16, 128, 128)]
    return tuple(outs)
```

```python
import contextlib
import numpy as np
import concourse.bass as bass
import concourse.mybir as mybir
from concourse.bass_utils import run_bass_kernel_spmd

F32 = mybir.dt.float32
BF16 = mybir.dt.bfloat16
F32R = mybir.dt.float32r


def r32(ap):
    return ap.bitcast(F32R)
AF = mybir.ActivationFunctionType
ALU = mybir.AluOpType

D = 2048
KC = 16
NIN = 17472
OFF_ZA, OFF_XBC, OFF_DT, OFF_QKV, OFF_ZB, OFF_BETA, OFF_A, OFF_GA, OFF_GB = (
    0, 2048, 5120, 5152, 11296, 13344, 13360, 13376, 15424)
EPS = 1e-6
NCH = 2
NEG = -1.0e30
NRING = 3
import os
KSK = os.environ.get('KSK', '')


class TK:
    NDMA = 24

    def __init__(self, nc, es):
        self.nc = nc
        self.eng = {"pe": nc.tensor, "act": nc.scalar, "dve": nc.vector, "pool": nc.gpsimd, "sp": nc.sync}
        self.sem = {k: es.enter_context(nc.semaphore("s_" + k)) for k in ("pe", "act", "dve", "pool")}
        self.cnt = {k: 0 for k in self.sem}
        self.dsem = [es.enter_context(nc.semaphore("d%d" % i)) for i in range(self.NDMA)]
        self.dcnt = 0
        self.known = {k: {} for k in self.eng}
        self.lastw = {}
        self.readers = {}

    def _wait(self, stream, tok):
        sem, val = tok[0], tok[1]
        kn = self.known[stream]
        if kn.get(id(sem), 0) >= val:
            return
        self.eng[stream].wait_ge(sem, val)
        kn[id(sem)] = val

    def _deps(self, stream, reads, writes, is_dma):
        toks = []
        for k in reads:
            lw = self.lastw.get(k)
            if lw is not None:
                toks.append(lw)
        for k in writes:
            lw = self.lastw.get(k)
            if lw is not None:
                toks.append(lw)
            toks.extend(self.readers.get(k, ()))
        for t in toks:
            if (not is_dma) and t[2] == stream and stream == "pe":
                continue
            self._wait(stream, t)

    def _commit(self, tok, reads, writes):
        for k in reads:
            lst = self.readers.setdefault(k, [])
            if tok[2] != "dma":
                lst[:] = [r for r in lst if r[2] != tok[2]]
            lst.append(tok)
        for k in writes:
            self.lastw[k] = tok
            self.readers[k] = []

    def op(self, stream, reads, writes, fn):
        bk = [k for k in reads if isinstance(k, str) and len(k) == 2 and k[0] == "b" and k[1].isdigit()]
        if bk:
            reads = [k for k in reads if k not in bk]
            writes = list(writes) + bk
        self._deps(stream, reads, writes, False)
        ins = fn()
        self.cnt[stream] += 1
        ins.then_inc(self.sem[stream], 1)
        tok = (self.sem[stream], self.cnt[stream], stream)
        self._commit(tok, reads, writes)
        return tok

    def dma(self, out, in_, reads, writes, stream="sp"):
        j = self.dcnt
        self.dcnt += 1
        sem = self.dsem[j % self.NDMA]
        prev = 16 * (j // self.NDMA)
        if prev > 0:
            self._wait(stream, (sem, prev))
        self._deps(stream, reads, writes, True)
        self.eng[stream].dma_start(out=out, in_=in_).then_inc(sem, 16)
        tok = (sem, prev + 16, "dma")
        self._commit(tok, reads, writes)
        return tok

    def finish(self):
        for k in ("pe", "act", "dve", "pool"):
            if self.cnt[k] > 0:
                self._wait("sp", (self.sem[k], self.cnt[k]))
        for i, sem in enumerate(self.dsem):
            n = (self.dcnt - i + self.NDMA - 1) // self.NDMA if self.dcnt > i else 0
            if n > 0:
                self._wait("sp", (sem, 16 * n))


def bc(ap, shape):
    return ap.broadcast_to(list(shape))


class _Stop(Exception):
    pass


def build_program(stop=None):
    nc = bass.Bass("TRN2", target_bir_lowering=False)

    def checkpoint(name):
        if stop == name:
            raise _Stop()

    def din(name, shape):
        return nc.dram_tensor(name, list(shape), F32, kind="ExternalInput").ap()

    def dout(name, shape):
        return nc.dram_tensor(name, list(shape), F32, kind="ExternalOutput").ap()

    xp = din("xp", [2, 2048, D]); xs = din("xs", [4, 64, D]); meta = din("meta", [16, D])
    sconvA = din("sconvA", [4, 3, 3072]); sA = din("sA", [4, 2048, 128])
    sconvB = din("sconvB", [4, 3, 6144]); sB = din("sB", [4, 2048, 128])
    w_in = din("w_in", [D, NIN]); w_out = din("w_out", [4096, D]); w_up = din("w_up", [D, 8192]); w_down = din("w_down", [8192, D])
    p_normmix = din("p_normmix", [1, D]); p_convwA = din("p_convwA", [4, 3072]); p_convbA = din("p_convbA", [1, 3072])
    p_dtbA = din("p_dtbA", [32, 1]); p_alogA = din("p_alogA", [1, 32]); p_dA = din("p_dA", [1, 32]); p_normA = din("p_normA", [1, D])
    p_convwB = din("p_convwB", [4, 6144]); p_dtbB = din("p_dtbB", [16, 1]); p_alogB = din("p_alogB", [16, 1]); p_normB = din("p_normB", [128, 1])
    p_normmlp = din("p_normmlp", [1, D]); p_normf = din("p_normf", [1, D])

    yp = dout("yp", [2, 2048, D]); ys = dout("ys", [4, 64, D])
    o_convA_p = dout("o_convA_p", [2, 3, 3072]); o_sA_p = dout("o_sA_p", [2, 2048, 128])
    o_convB_p = dout("o_convB_p", [2, 3, 6144]); o_sB_p = dout("o_sB_p", [2, 2048, 128])
    o_convA_s = dout("o_convA_s", [4, 3, 3072]); o_sA_s = dout("o_sA_s", [4, 2048, 128])
    o_convB_s = dout("o_convB_s", [4, 3, 6144]); o_sB_s = dout("o_sB_s", [4, 2048, 128])

    NCHUNK = 148
    wsc = nc.dram_tensor("wsc", [NCHUNK, 128, 4096], BF16, kind="Internal").ap()

    TMAX = 2 * NCH * 64
    NBLK = TMAX // 128

    with contextlib.ExitStack() as es:
        tk = TK(nc, es)

        def sb(name, shape, dt=F32):
            return es.enter_context(nc.sbuf_tensor(name, list(shape), dt))

        PSA = es.enter_context(nc.psum_tensor("PSA", [128, 4096], F32))

        def bank(i):
            return PSA[:, 512 * i:512 * (i + 1)]
        pTb = PSA[:, 0:1024].bitcast(BF16)

        R1W = TMAX * 16 + 64
        R1 = sb("R1", [128, R1W])
        R2 = sb("R2", [128, 12 * TMAX + 64])
        POSTS = [sb("post0", [128, 6 * TMAX]), sb("post1", [128, 6 * TMAX])]
        SQ = sb("SQ", [128, 4 * TMAX])
        uT = sb("uT", [128, KC, TMAX], BF16)
        yaT = sb("yaT", [128, KC, TMAX], BF16)
        obT = sb("obT", [128, KC, TMAX], BF16)
        ring = [sb("ring%d" % i, [128, 4096], BF16) for i in range(NRING)]
        hTs = [sb("hT%d" % q, [128, 2048]) for q in range(2)]
        Ss = [sb("S%d" % q, [128, 16, 128]) for q in range(2)]
        EP = sb("EP", [128, max(TMAX * 9 + 16, 4096)])
        CTR_ = [sb("CTR%d" % q, [128, 2816]) for q in range(2)]
        CTN_ = [sb("CTN%d" % q, [128, 2048]) for q in range(2)]
        SMT = sb("SMT", [128, 2, 256])
        SM128 = sb("SM128", [128, 2, 48])
        SMG = sb("SMG", [128, 2, 96]); SMG128 = sb("SMG128", [128, 2, 16])
        smF = sb("smF", [32, 3, TMAX])
        xn = sb("xn", [128, D], BF16)
        stat = sb("stat", [128, 8])
        wsm = sb("wsm", [128, KC, 64], BF16)
        identF = sb("identF", [128, 128]); identB = sb("identB", [128, 128], BF16)
        onesF = sb("onesF", [128, 128]); onesT = sb("onesT", [128, 128])
        triu = sb("triu", [128, 128]); maskU = sb("maskU", [128, 128]); maskL = sb("maskL", [128, 128]); mask01s = sb("mask01s", [128, 128])
        normmix_fm = sb("normmix_fm", [128, 16]); normmlp_fm = sb("normmlp_fm", [128, 16]); normA_fm = sb("normA_fm", [128, 16])
        convwA = sb("convwA", [128, 24, 4]); convbA = sb("convbA", [128, 24]); convwB = sb("convwB", [128, 48, 4])
        tailA = sb("tailA", [128, 24, 6]); tailB = sb("tailB", [128, 48, 6])
        a_bc = sb("a_bc", [128, 32]); D_bc = sb("D_bc", [128, 32]); D_fm = sb("D_fm", [128, 16])
        dtbA_col = sb("dtbA_col", [32, 1]); dtbB_col = sb("dtbB_col", [16, 1]); negaB_col = sb("negaB_col", [16, 1])
        normB_col = sb("normB_col", [128, 1])
        normf_bc = sb("normf_bc", [128, D])
        dummy = sb("dummyt", [128, 2])

        def V(r, w, f): return tk.op("dve", r, w, f)
        def A(r, w, f): return tk.op("act", r, w, f)
        def G(r, w, f): return tk.op("pool", r, w, f)
        def PE(r, w, f): return tk.op("pe", r, w, f)
        def BAR(r, w): return tk.op("pool", [], list(r) + list(w) + ["dummy"], lambda: nc.gpsimd.memset(dummy[:], 0.0))

        G([], ["identF"], lambda: nc.gpsimd.memset(identF[:], 0.0))
        G(["identF"], ["identF"], lambda: nc.gpsimd.affine_select(out=identF[:], in_=identF[:], compare_op=ALU.not_equal, fill=1.0, base=0, pattern=[[-1, 128]], channel_multiplier=1))
        G(["identF"], ["identB"], lambda: nc.gpsimd.tensor_copy(out=identB[:], in_=identF[:]))
        G([], ["onesT"], lambda: nc.gpsimd.memset(onesT[:], 1.0))
        V(["onesT"], ["onesF"], lambda: nc.vector.tensor_copy(out=r32(onesF[:]), in_=onesT[:]))
        G([], ["triu"], lambda: nc.gpsimd.memset(triu[:], 1.0))
        G(["triu"], ["triu"], lambda: nc.gpsimd.affine_select(out=triu[:], in_=triu[:], compare_op=ALU.is_ge, fill=0.0, base=0, pattern=[[1, 128]], channel_multiplier=-1))
        G([], ["maskU"], lambda: nc.gpsimd.memset(maskU[:], 0.0))
        G(["maskU"], ["maskU"], lambda: nc.gpsimd.affine_select(out=maskU[:], in_=maskU[:], compare_op=ALU.is_ge, fill=NEG, base=0, pattern=[[1, 128]], channel_multiplier=-1))
        G([], ["maskL"], lambda: nc.gpsimd.memset(maskL[:], 0.0))
        G(["maskL"], ["maskL"], lambda: nc.gpsimd.affine_select(out=maskL[:], in_=maskL[:], compare_op=ALU.is_ge, fill=NEG, base=0, pattern=[[-1, 128]], channel_multiplier=1))
        G([], ["mask01s"], lambda: nc.gpsimd.memset(mask01s[:], 1.0))
        G(["mask01s"], ["mask01s"], lambda: nc.gpsimd.affine_select(out=mask01s[:], in_=mask01s[:], compare_op=ALU.is_gt, fill=0.0, base=0, pattern=[[-1, 128]], channel_multiplier=1))
        BAR([], ["dummy"])

        R1_KEYS = ["xt", "pre", "post", "zS", "stg"]
        EP_ALL = [("ySB", 0), ("ySB", 1), "sqE", "rnE", "sa", "sb", "EPst"]

        ALL_BANK_KEYS = ["b%d" % i for i in range(8)]

        def bank_barrier():
            BAR([], ALL_BANK_KEYS)

        def feat_from_rows(src, R, C, dest, dkey):
            for p0 in range(0, C, 2048):
                pc = min(2048, C - p0)
                n = pc // 128
                c0 = p0 // 128
                tk.dma(R1[0:R, 0:pc], src[:, p0:p0 + pc], [], R1_KEYS)
                def f():
                    r = None
                    for i in range(n):
                        r = nc.tensor.transpose(bank(2)[:, i * R:(i + 1) * R], R1[0:R, i * 128:(i + 1) * 128], identF[0:R, 0:R])
                    return r
                PE(R1_KEYS + ["identF"], ["b2"], f)
                V(["b2"], [dkey], lambda: nc.vector.tensor_copy(out=dest[:, c0:c0 + n, :], in_=bank(2)[:, 0:n * R].rearrange("p (c r) -> p c r", r=R)))

        def rows_from_feat(srcT, skey, R, C, dst):
            for p0 in range(0, C, 2048):
                pc = min(2048, C - p0)
                cb = p0 // 128
                for c0 in range(0, pc // 128, 4):
                    def f():
                        r = None
                        for i in range(4):
                            r = nc.tensor.transpose(bank(2)[0:R, i * 128:(i + 1) * 128], srcT[:, cb + c0 + i, :], identF[:, :])
                        return r
                    PE([skey, "identF"], ["b2"], f)
                    V(["b2"], R1_KEYS, lambda: nc.vector.tensor_copy(out=R1[0:R, c0 * 128:(c0 + 4) * 128], in_=bank(2)[0:R, 0:512]))
                tk.dma(dst[:, p0:p0 + pc], R1[0:R, 0:pc], R1_KEYS, [])

        feat_from_rows(p_normmix, 1, D, normmix_fm[:, :].unsqueeze(2), "normmix_fm")
        feat_from_rows(p_normmlp, 1, D, normmlp_fm[:, :].unsqueeze(2), "normmlp_fm")
        feat_from_rows(p_normA, 1, D, normA_fm[:, :].unsqueeze(2), "normA_fm")
        feat_from_rows(p_convwA, 4, 3072, convwA[:, :, :], "convwA")
        feat_from_rows(p_convbA, 1, 3072, convbA[:, :].unsqueeze(2), "convbA")
        feat_from_rows(p_convwB, 4, 6144, convwB[:, :, :], "convwB")

        tk.dma(a_bc[:], bc(p_alogA[0:1, :], [128, 32]), [], ["a_bc"])
        A(["a_bc"], ["a_bc"], lambda: nc.scalar.activation(out=a_bc[:], in_=a_bc[:], func=AF.Exp))
        V(["a_bc"], ["a_bc"], lambda: nc.vector.tensor_scalar(out=a_bc[:], in0=a_bc[:], scalar1=-1.0, scalar2=None, op0=ALU.mult))
        tk.dma(D_bc[:], bc(p_dA[0:1, :], [128, 32]), [], ["D_bc"])
        Dv = D_bc[:, :].rearrange("p (c two) -> p c two", two=2)
        V(["D_bc"], ["D_fm"], lambda: nc.vector.tensor_copy(out=D_fm[0:64, :], in_=Dv[0:64, :, 0]))
        V(["D_bc"], ["D_fm"], lambda: nc.vector.tensor_copy(out=D_fm[64:128, :], in_=Dv[64:128, :, 1]))
        tk.dma(dtbA_col[:], p_dtbA[:, :], [], ["dtbA_col"])
        tk.dma(dtbB_col[:], p_dtbB[:, :], [], ["dtbB_col"])
        tk.dma(negaB_col[:], p_alogB[:, :], [], ["negaB_col"])
        A(["negaB_col"], ["negaB_col"], lambda: nc.scalar.activation(out=negaB_col[:], in_=negaB_col[:], func=AF.Exp))
        V(["negaB_col"], ["negaB_col"], lambda: nc.vector.tensor_scalar(out=negaB_col[:], in0=negaB_col[:], scalar1=-1.0, scalar2=None, op0=ALU.mult))
        tk.dma(normB_col[:], p_normB[:, :], [], ["normB_col"])
        tk.dma(normf_bc[:], bc(p_normf[0:1, :], [128, D]), [], ["normf_bc"])

        chunks = []
        for g in range(4):
            chunks.append(("b", [OFF_ZA + 512 * g, OFF_ZA + 512 * g + 128]))
            chunks.append(("b", [OFF_ZA + 512 * g + 256, OFF_ZA + 512 * g + 384]))
            chunks.append(("b", [OFF_XBC + 512 * g, OFF_XBC + 512 * g + 128]))
            chunks.append(("b", [OFF_XBC + 512 * g + 256, OFF_XBC + 512 * g + 384]))
            chunks.append(("b", [OFF_XBC + 2048 + 128 * g, OFF_XBC + 2560 + 128 * g]))
        for j in range(8):
            chunks.append(("b", [OFF_QKV + 256 * j, OFF_QKV + 256 * j + 128]))
            chunks.append(("b", [OFF_QKV + 2048 + 256 * j, OFF_QKV + 2048 + 256 * j + 128]))
            chunks.append(("b", [OFF_QKV + 4096 + 256 * j, OFF_QKV + 4096 + 256 * j + 128]))
            chunks.append(("b", [OFF_ZB + 256 * j, OFF_ZB + 256 * j + 128]))
        for cg in range(4):
            for off in (OFF_GA, OFF_GB):
                for kh in range(2):
                    chunks.append(("a", w_in, kh * 8, off + 512 * cg))
            for kq in range(4):
                chunks.append(("a", w_out, kq * 8, 512 * cg))
        for fb in range(4):
            for c in range(8):
                chunks.append(("bu", 2048 * fb + 256 * c))
            for cg in range(4):
                for kh in range(2):
                    chunks.append(("a", w_down, fb * 16 + kh * 8, 512 * cg))
        assert len(chunks) == NCHUNK

        BAR(R1_KEYS, ["stgF0", "stgF1", "stgB0", "stgB1"])
        stgF = [R1[:, 0:4096], EP[:, 0:4096]]
        stgB = [yaT[:, :, :].rearrange("p a b -> p (a b)")[:, 0:4096], obT[:, :, :].rearrange("p a b -> p (a b)")[:, 0:4096]]
        cast_eng = ["pool", "act", "dve"]
        for ci, ch in enumerate(chunks):
            sf = stgF[ci % 2]; sbf = stgB[ci % 2]
            kf = "stgF%d" % (ci % 2); kb = "stgB%d" % (ci % 2)
            if ch[0] == "b" and ch[1][1] == ch[1][0] + 128:
                off = ch[1][0]
                dstv = sf[:, 0:4096].rearrange("p (k c) -> p k c", c=256)
                tk.dma(dstv, w_in[:, off:off + 256].rearrange("(k p) c -> p k c", p=128), [], [kf])
                nel = 4096
            elif ch[0] == "b":
                for hlf in range(2):
                    off = ch[1][hlf]
                    dstv = sf[:, 0:4096].rearrange("p (k c) -> p k c", c=256)[:, :, hlf * 128:(hlf + 1) * 128]
                    tk.dma(dstv, w_in[:, off:off + 128].rearrange("(k p) c -> p k c", p=128), [], [kf])
                nel = 4096
            elif ch[0] == "bu":
                off = ch[1]
                dstv = sf[:, 0:4096].rearrange("p (k c) -> p k c", c=256)
                tk.dma(dstv, w_up[:, off:off + 256].rearrange("(k p) c -> p k c", p=128), [], [kf])
                nel = 4096
            else:
                Wt, r0, c0 = ch[1], ch[2], ch[3]
                dstv = sf[:, 0:4096].rearrange("p (k c) -> p k c", c=512)
                tk.dma(dstv, Wt[r0 * 128:(r0 + 8) * 128, c0:c0 + 512].rearrange("(k p) c -> p k c", p=128), [], [kf])
                nel = 4096
            e = cast_eng[ci % 3]
            if e == "pool":
                G([kf], [kb], lambda: nc.gpsimd.tensor_copy(out=sbf, in_=sf[:, 0:4096]))
            elif e == "act":
                A([kf], [kb], lambda: nc.scalar.copy(out=sbf, in_=sf[:, 0:4096]))
            else:
                V([kf], [kb], lambda: nc.vector.tensor_copy(out=sbf, in_=sf[:, 0:4096]))
            tk.dma(wsc[ci], sbf, [kb], [("wsc", ci)])
        st = R1[:, 0:1024].rearrange("p (k c) -> p k c", c=64)
        tk.dma(st[:, :, 0:32], w_in[:, OFF_DT:OFF_DT + 32].rearrange("(k p) c -> p k c", p=128), [], ["stgF0"])
        tk.dma(st[:, :, 32:64], w_in[:, OFF_BETA:OFF_BETA + 32].rearrange("(k p) c -> p k c", p=128), [], ["stgF0"])
        V(["stgF0"], ["wsm"], lambda: nc.vector.tensor_copy(out=wsm[:], in_=st))
        BAR(["stgF0", "stgF1", "stgB0", "stgB1"], R1_KEYS + EP_ALL + ["yaT", "obT"])

        wstate = {"issued": 0, "total": 0}

        def wissue(upto):
            while wstate["issued"] <= upto and wstate["issued"] < wstate["total"]:
                i = wstate["issued"]
                b = i % NRING
                tk.dma(ring[b][:, :], wsc[i % NCHUNK], [("wsc", i % NCHUNK)], [("ring", b)])
                wstate["issued"] += 1

        wcur = {"i": 0}

        def wnext():
            i = wcur["i"]
            wissue(i + NRING - 1)
            wcur["i"] += 1
            b = i % NRING
            return ring[b], ("ring", b)

        def process_tile(nch, Lc, blocks, seq_out, kind):
            W = nch * Lc
            T = 2 * W
            nslot = 2 * nch
            Lg = min(128, W)
            nchg = W // Lg
            LVG = {128: 6, 64: 5, 16: 3}[Lg]
            xt = R1[:, 0:len(blocks) * D].rearrange("p (b d) -> p b d", d=D)
            pre = R1[:, 0:6 * 2 * (W + 3)].rearrange("p (i s w) -> p i s w", i=6, s=2)
            o1 = 6 * 2 * (W + 3)
            o2 = o1 + 2 * T
            zS = R1[:, o2:o2 + 4 * T].rearrange("p (i t) -> p i t", i=4)
            assert o2 + 4 * T <= R1W
            ySB = EP[:, 0:4 * T].rearrange("p (i t) -> p i t", i=4)
            sqE = SQ[:, 0:4 * T].rearrange("p (i t) -> p i t", i=4)
            rnE = EP[:, 4 * T:5 * T]

            def load_x():
                for b, (col0, srcs, _o) in enumerate(blocks):
                    for (src, p0, n) in srcs:
                        tk.dma(xt[p0:p0 + n, b, :], src, [], R1_KEYS)

            def norm_to_T(wfm, wkey):
                for b, (col0, srcs, _o) in enumerate(blocks):
                    n = sum(s[2] for s in srcs)
                    A(["xt"], ["xn", "stat"], lambda: nc.scalar.activation(out=xn[0:n, :], in_=xt[0:n, b, :], func=AF.Square, accum_out=stat[0:n, 0:1]))
                    A(["stat"], ["stat"], lambda: nc.scalar.activation(out=stat[0:n, 1:2], in_=stat[0:n, 0:1], func=AF.Ln, scale=1.0 / D, bias=EPS))
                    A(["stat"], ["stat"], lambda: nc.scalar.activation(out=stat[0:n, 2:3], in_=stat[0:n, 1:2], func=AF.Exp, scale=-0.5))
                    A(["xt", "stat"], ["xn"], lambda: nc.scalar.activation(out=xn[0:n, :], in_=xt[0:n, b, :], func=AF.Copy, scale=stat[0:n, 2:3]))
                    def f():
                        r = None
                        for kc in range(KC):
                            r = nc.tensor.transpose(pTb[:, kc * 128:kc * 128 + n], xn[0:n, kc * 128:(kc + 1) * 128], identB[0:n, 0:n])
                        return r
                    PE(["xn", "identB"], ["b0", "b1"], f)
                    V(["b0", "b1", wkey], ["uT"], lambda: nc.vector.tensor_tensor(
                        out=uT[:, :, col0:col0 + n], in0=pTb[:, :].rearrange("p (k c) -> p k c", c=128)[:, :, 0:n],
                        in1=bc(wfm[:, :].unsqueeze(2), [128, KC, n]), op=ALU.mult))

            load_x()
            norm_to_T(normmix_fm, "normmix_fm")
            BAR(["xt"], ["pre", "post", "zS"])

            checkpoint("phase0")
            def fsm():
                r = None
                for (bk, c0, m) in ((2, 0, 32), (3, 32, 16), (4, 48, 16)):
                    for kc in range(KC):
                        r = nc.tensor.matmul(bank(bk)[0:m, 0:T], lhsT=wsm[:, kc, c0:c0 + m], rhs=uT[:, kc, 0:T], start=(kc == 0), stop=(kc == KC - 1))
                return r
            PE(["uT", "wsm"], ["b2", "b3", "b4"], fsm)
            A(["b2", "dtbA_col"], ["smF"], lambda: nc.scalar.activation(out=smF[0:32, 0, 0:T], in_=bank(2)[0:32, 0:T], func=AF.Exp, bias=dtbA_col[:, 0:1]))
            A(["smF"], ["smF"], lambda: nc.scalar.activation(out=smF[0:32, 0, 0:T], in_=smF[0:32, 0, 0:T], func=AF.Ln, bias=1.0))
            A(["b3"], ["smF"], lambda: nc.scalar.activation(out=smF[0:16, 1, 0:T], in_=bank(3)[0:16, 0:T], func=AF.Sigmoid))
            A(["b4", "dtbB_col"], ["smF"], lambda: nc.scalar.activation(out=smF[0:16, 2, 0:T], in_=bank(4)[0:16, 0:T], func=AF.Exp, bias=dtbB_col[:, 0:1]))
            A(["smF"], ["smF"], lambda: nc.scalar.activation(out=smF[0:16, 2, 0:T], in_=smF[0:16, 2, 0:T], func=AF.Ln, bias=1.0))
            V(["smF", "negaB_col"], ["smF"], lambda: nc.vector.tensor_scalar(out=smF[0:16, 2, 0:T], in0=smF[0:16, 2, 0:T], scalar1=negaB_col[:, 0:1], scalar2=None, op0=ALU.mult))
            for q in range(2):
                for c in range(nch):
                    sl = q * nch + c
                    cols = slice(q * W + c * Lc, q * W + (c + 1) * Lc)
                    sk = ("smt", sl)
                    def f():
                        nc.tensor.transpose(bank(2)[0:Lc, 0:32], smF[0:32, 0, cols], identF[0:32, 0:32])
                        nc.tensor.transpose(bank(2)[0:Lc, 32:48], smF[0:16, 1, cols], identF[0:16, 0:16])
                        return nc.tensor.transpose(bank(2)[0:Lc, 48:64], smF[0:16, 2, cols], identF[0:16, 0:16])
                    PE(["smF", "identF"], ["b2"], f)
                    V(["b2"], [sk], lambda: nc.vector.tensor_copy(out=SMT[0:Lc, sl, 0:32], in_=bank(2)[0:Lc, 0:32]))
                    V(["b2"], [sk], lambda: nc.vector.tensor_copy(out=SMT[0:Lc, sl, 160:192], in_=bank(2)[0:Lc, 32:64]))
                    V([sk, "a_bc"], [sk], lambda: nc.vector.tensor_tensor(out=SMT[0:Lc, sl, 32:64], in0=SMT[0:Lc, sl, 0:32], in1=a_bc[0:Lc, :], op=ALU.mult))
                    def f2():
                        nc.tensor.matmul(bank(3)[0:Lc, 0:32], lhsT=triu[0:Lc, 0:Lc], rhs=SMT[0:Lc, sl, 32:64], start=True, stop=True)
                        nc.tensor.matmul(bank(3)[0:Lc, 32:48], lhsT=triu[0:Lc, 0:Lc], rhs=SMT[0:Lc, sl, 176:192], start=True, stop=True)
                        nc.tensor.matmul(bank(3)[:, 64:96], lhsT=onesF[0:Lc, :], rhs=SMT[0:Lc, sl, 32:64], start=True, stop=True)
                        return nc.tensor.matmul(bank(3)[:, 96:112], lhsT=onesF[0:Lc, :], rhs=SMT[0:Lc, sl, 176:192], start=True, stop=True)
                    PE([sk, "triu", "onesF"], ["b3"], f2)
                    V(["b3"], [sk], lambda: nc.vector.tensor_copy(out=SMT[0:Lc, sl, 64:96], in_=bank(3)[0:Lc, 0:32]))
                    V(["b3"], [sk], lambda: nc.vector.tensor_copy(out=SMT[0:Lc, sl, 192:208], in_=bank(3)[0:Lc, 32:48]))
                    A(["b3"], [("sm128", sl)], lambda: nc.scalar.activation(out=SM128[:, sl, 0:48], in_=bank(3)[:, 64:112], func=AF.Exp))
                    V(["b3", sk], [sk], lambda: nc.vector.tensor_tensor(out=SMT[0:Lc, sl, 96:128], in0=bank(3)[0:Lc, 64:96], in1=SMT[0:Lc, sl, 64:96], op=ALU.subtract))
                    V(["b3", sk], [sk], lambda: nc.vector.tensor_tensor(out=SMT[0:Lc, sl, 208:224], in0=bank(3)[0:Lc, 96:112], in1=SMT[0:Lc, sl, 192:208], op=ALU.subtract))
                    A([sk], [sk], lambda: nc.scalar.activation(out=SMT[0:Lc, sl, 96:128], in_=SMT[0:Lc, sl, 96:128], func=AF.Exp))
                    A([sk], [sk], lambda: nc.scalar.activation(out=SMT[0:Lc, sl, 208:224], in_=SMT[0:Lc, sl, 208:224], func=AF.Exp))
                    A([sk], [sk], lambda: nc.scalar.activation(out=SMT[0:Lc, sl, 128:160], in_=SMT[0:Lc, sl, 64:96], func=AF.Exp))
                    A([sk], [sk], lambda: nc.scalar.activation(out=SMT[0:Lc, sl, 224:240], in_=SMT[0:Lc, sl, 192:208], func=AF.Exp))
                    V([sk], [sk], lambda: nc.vector.tensor_tensor(out=SMT[0:Lc, sl, 224:240], in0=SMT[0:Lc, sl, 224:240], in1=SMT[0:Lc, sl, 160:176], op=ALU.mult))
                    V([sk], [sk], lambda: nc.vector.tensor_scalar(out=SMT[0:Lc, sl, 240:256], in0=SMT[0:Lc, sl, 160:176], scalar1=-1.0, scalar2=None, op0=ALU.mult))

            for q in range(2):
                for c in range(nchg):
                    sl = q * nchg + c
                    cols = slice(q * W + c * Lg, q * W + (c + 1) * Lg)
                    sk = ("smg", sl)
                    def f():
                        nc.tensor.transpose(bank(2)[0:Lg, 0:16], smF[0:16, 1, cols], identF[0:16, 0:16])
                        return nc.tensor.transpose(bank(2)[0:Lg, 16:32], smF[0:16, 2, cols], identF[0:16, 0:16])
                    PE(["smF", "identF"], ["b2"], f)
                    V(["b2"], [sk], lambda: nc.vector.tensor_copy(out=SMG[0:Lg, sl, 0:32], in_=bank(2)[0:Lg, 0:32]))
                    def f2():
                        nc.tensor.matmul(bank(3)[0:Lg, 0:16], lhsT=triu[0:Lg, 0:Lg], rhs=SMG[0:Lg, sl, 16:32], start=True, stop=True)
                        return nc.tensor.matmul(bank(3)[:, 32:48], lhsT=onesF[0:Lg, :], rhs=SMG[0:Lg, sl, 16:32], start=True, stop=True)
                    PE([sk, "triu", "onesF"], ["b3"], f2)
                    V(["b3"], [sk], lambda: nc.vector.tensor_copy(out=SMG[0:Lg, sl, 32:48], in_=bank(3)[0:Lg, 0:16]))
                    A(["b3"], [("smg128", sl)], lambda: nc.scalar.activation(out=SMG128[:, sl, 0:16], in_=bank(3)[:, 32:48], func=AF.Exp))
                    V(["b3", sk], [sk], lambda: nc.vector.tensor_tensor(out=SMG[0:Lg, sl, 48:64], in0=bank(3)[0:Lg, 32:48], in1=SMG[0:Lg, sl, 32:48], op=ALU.subtract))
                    A([sk], [sk], lambda: nc.scalar.activation(out=SMG[0:Lg, sl, 48:64], in_=SMG[0:Lg, sl, 48:64], func=AF.Exp))
                    A([sk], [sk], lambda: nc.scalar.activation(out=SMG[0:Lg, sl, 64:80], in_=SMG[0:Lg, sl, 32:48], func=AF.Exp))
                    V([sk], [sk], lambda: nc.vector.tensor_tensor(out=SMG[0:Lg, sl, 64:80], in0=SMG[0:Lg, sl, 64:80], in1=SMG[0:Lg, sl, 0:16], op=ALU.mult))
                    V([sk], [sk], lambda: nc.vector.tensor_scalar(out=SMG[0:Lg, sl, 80:96], in0=SMG[0:Lg, sl, 0:16], scalar1=-1.0, scalar2=None, op0=ALU.mult))
            checkpoint("small")

            pm = {"i": 0}

            def proj_b(wbuf, wkey, half, banks=(0, 1, 2, 3, 4, 5, 6, 7)):
                bk = banks[pm["i"] % len(banks)]
                pm["i"] += 1
                def f():
                    r = None
                    for kc in range(KC):
                        r = nc.tensor.matmul(bank(bk)[:, 0:T], lhsT=wbuf[:, kc * 256 + half * 128: kc * 256 + (half + 1) * 128], rhs=uT[:, kc, 0:T], start=(kc == 0), stop=(kc == KC - 1))
                    return r
                PE(["uT", wkey], ["b%d" % bk], f)
                return bk

            def make_set(bs, Rf):
                pre = Rf[:, 0:6 * 2 * (W + 3)].rearrange("p (i s w) -> p i s w", i=6, s=2)
                post = POSTS[bs][:, 0:6 * T].rearrange("p (i t) -> p i t", i=6)
                zS = Rf[:, o2:o2 + 4 * T].rearrange("p (i t) -> p i t", i=4)
                PRE, POST, ZS = ("pre", "post", "zS") if bs == 0 else ("pre1", "post1", "zS1")
                def conv_tile(i, cw, cb, ct, tail, tkey):
                    pk = ("pre", bs, i); ok = ("post", bs, i); ck = ("cacc", bs, i % 2)
                    cflat = Rf[:, o1 + (i % 2) * T:o1 + (i % 2 + 1) * T]
                    pv = cflat.rearrange("p (s w) -> p s w", s=2)
                    if cb is not None:
                        V([PRE, pk], [ck], lambda: nc.vector.tensor_scalar(out=pv, in0=pre[:, i, :, 0:W], scalar1=cw[:, ct, 0:1], scalar2=cb[:, ct:ct + 1], op0=ALU.mult, op1=ALU.add))
                    else:
                        V([PRE, pk], [ck], lambda: nc.vector.tensor_scalar(out=pv, in0=pre[:, i, :, 0:W], scalar1=cw[:, ct, 0:1], scalar2=None, op0=ALU.mult))
                    for k in range(1, 4):
                        V([PRE, pk, ck], [ck], lambda: nc.vector.scalar_tensor_tensor(out=pv, in0=pre[:, i, :, k:k + W], scalar=cw[:, ct, k:k + 1], in1=pv, op0=ALU.mult, op1=ALU.add))
                    A([ck], [ok], lambda: nc.scalar.activation(out=r32(post[:, i, :]), in_=cflat, func=AF.Silu))
                    G([PRE, pk], [tkey], lambda: nc.gpsimd.tensor_copy(out=tail[:, ct, :].rearrange("p (s r) -> p s r", s=2), in_=pre[:, i, :, W:W + 3]))

                def load_pre(i, bk, ct, tail, tkey):
                    pk = ("pre", bs, i)
                    G([tkey, PRE], [pk], lambda: nc.gpsimd.tensor_copy(out=pre[:, i, :, 0:3], in_=tail[:, ct, :].rearrange("p (s r) -> p s r", s=2)))
                    A(["b%d" % bk, PRE], [pk], lambda: nc.scalar.copy(out=pre[:, i, :, 3:3 + W], in_=bank(bk)[:, 0:T].rearrange("p (s w) -> p s w", s=2)))

                def runpair(gens):
                    if 'n' in KSK:
                        return
                    live = list(gens)
                    while live:
                        for gq in list(live):
                            try:
                                next(gq)
                            except StopIteration:
                                live.remove(gq)

                def ssd_chunk(g, q, c):
                    sl = q * nch + c
                    cols = slice(q * W + c * Lc, q * W + (c + 1) * Lc)
                    sk = ("smt", sl)
                    ia, ib, ic = ((2, 3, 4), (7, 6, 5))[q]
                    Aq, Bq, Cq = bank(ia), bank(ib), bank(ic)
                    ka, kb_, kc_ = "b%d" % ia, "b%d" % ib, "b%d" % ic
                    TR = CTR_[q]; TN = CTN_[q]
                    tkq = lambda nm: ("ct", q, nm)
                    H4 = 4 * Lc
                    xdt = TR[0:Lc, 0:512].rearrange("p (h x) -> p h x", h=8)
                    Btok = TR[0:Lc, 512:640]
                    MTs = [TR[0:Lc, 640 + hh * 512:640 + hh * 512 + H4].rearrange("p (h t) -> p h t", h=4) for hh in range(2)]
                    xdtw = TR[0:Lc, 1664:2176]
                    ATf = TN[0:Lc, 0:H4]
                    AT = ATf.rearrange("p (h t) -> p h t", h=4)
                    LTf = TN[0:Lc, 512:512 + H4]
                    LT = LTf.rearrange("p (h t) -> p h t", h=4)
                    yv = TN[0:Lc, 1024:1536]
                    hs = TN[:, 1536:2048]
                    hst = hTs[q][:, g * 512:(g + 1) * 512]
                    hk = ("hT", q, g)
                    def f():
                        r = None
                        for i in range(4):
                            r = nc.tensor.transpose(Aq[0:Lc, i * 128:(i + 1) * 128], post[:, i, cols], identF[:, :])
                        r = nc.tensor.transpose(Bq[0:Lc, 0:128], post[:, 4, cols], identF[:, :])
                        r = nc.tensor.matmul(Bq[0:Lc, 128:128 + Lc], lhsT=r32(post[:, 4, cols]), rhs=r32(post[:, 5, cols]), start=True, stop=True)
                        return r
                    PE([("post", bs, i) for i in range(6)] + ["identF"], [ka, kb_], f)
                    V([ka, sk], [tkq("xdt")], lambda: nc.vector.tensor_tensor(out=r32(xdt), in0=Aq[0:Lc, 0:512].rearrange("p (h x) -> p h x", h=8), in1=bc(SMT[0:Lc, sl, 8 * g:8 * g + 8].unsqueeze(2), [Lc, 8, 64]), op=ALU.mult))
                    A([kb_], [tkq("Btok")], lambda: nc.scalar.copy(out=r32(Btok), in_=Bq[0:Lc, 0:128]))
                    A([kb_], [tkq("cbT"), tkq("hs")], lambda: nc.scalar.copy(out=TN[0:Lc, 1536:1536 + Lc], in_=Bq[0:Lc, 128:128 + Lc]))
                    cbT = TN[0:Lc, 1536:1536 + Lc]
                    for hh in range(2):
                        hb = 8 * g + 4 * hh
                        G([sk, "triu"], [tkq("AT")], lambda: nc.gpsimd.tensor_tensor(out=AT, in0=bc(SMT[0:Lc, sl, 32 + hb:36 + hb].unsqueeze(2), [Lc, 4, Lc]), in1=bc(triu[0:Lc, 0:Lc].unsqueeze(1), [Lc, 4, Lc]), op=ALU.mult))
                        PE([tkq("AT"), "onesF"], [kc_], lambda: nc.tensor.matmul(Cq[0:Lc, 0:H4], lhsT=onesF[0:Lc, 0:Lc], rhs=ATf, start=True, stop=True))
                        yield
                        V([kc_, sk], [tkq("LT")], lambda: nc.vector.tensor_tensor(out=LT, in0=Cq[0:Lc, 0:H4].rearrange("p (h t) -> p h t", h=4), in1=bc(SMT[0:Lc, sl, 64 + hb:68 + hb].unsqueeze(2), [Lc, 4, Lc]), op=ALU.subtract))
                        G([tkq("LT"), "maskU"], [tkq("LT")], lambda: nc.gpsimd.tensor_tensor(out=LT, in0=LT, in1=bc(maskU[0:Lc, 0:Lc].unsqueeze(1), [Lc, 4, Lc]), op=ALU.add))
                        A([tkq("LT")], [tkq("LT")], lambda: nc.scalar.activation(out=LTf, in_=LTf, func=AF.Exp))
                        G([tkq("LT"), tkq("cbT")], [tkq("MT%d" % hh)], lambda: nc.gpsimd.tensor_tensor(out=r32(MTs[hh]), in0=LT, in1=bc(cbT.unsqueeze(1), [Lc, 4, Lc]), op=ALU.mult))
                    def f():
                        r = None
                        for h in range(8):
                            r = nc.tensor.matmul(Aq[0:Lc, h * 64:(h + 1) * 64], lhsT=r32(MTs[h // 4][:, h % 4, :]), rhs=r32(xdt[:, h, :]), start=True, stop=True)
                        r = nc.tensor.matmul(Cq[0:Lc, 0:512], lhsT=post[:, 5, cols], rhs=hst, start=True, stop=True)
                        return r
                    PE([tkq("MT0"), tkq("MT1"), tkq("xdt"), ("post", bs, 5), hk], [ka, kc_], f)
                    yield
                    V([kc_, sk], [tkq("y")], lambda: nc.vector.tensor_tensor(out=yv.rearrange("p (h x) -> p h x", h=8), in0=Cq[0:Lc, 0:512].rearrange("p (h x) -> p h x", h=8), in1=bc(SMT[0:Lc, sl, 128 + 8 * g:136 + 8 * g].unsqueeze(2), [Lc, 8, 64]), op=ALU.mult))
                    V([ka, tkq("y")], [tkq("y")], lambda: nc.vector.tensor_tensor(out=yv, in0=yv, in1=Aq[0:Lc, 0:512], op=ALU.add))
                    G([tkq("xdt"), sk], [tkq("xdtw")], lambda: nc.gpsimd.tensor_tensor(out=r32(xdtw.rearrange("p (h x) -> p h x", h=8)), in0=xdt, in1=bc(SMT[0:Lc, sl, 96 + 8 * g:104 + 8 * g].unsqueeze(2), [Lc, 8, 64]), op=ALU.mult))
                    def f():
                        r = None
                        for i in range(4):
                            r = nc.tensor.transpose(Cq[:, i * Lc:(i + 1) * Lc], yv[:, i * 128:(i + 1) * 128], identF[0:Lc, 0:Lc])
                        r = nc.tensor.matmul(Aq[:, 0:512], lhsT=r32(Btok), rhs=r32(xdtw), start=True, stop=True)
                        return r
                    PE([tkq("y"), "identF", tkq("Btok"), tkq("xdtw")], [kc_, ka], f)
                    yield
                    A([kc_], [("ySB", q)], lambda: nc.scalar.copy(out=ySB[:, :, cols], in_=Cq[:, 0:4 * Lc].rearrange("p (i t) -> p i t", i=4)))
                    G([hk, ("sm128", sl)], [tkq("hs"), tkq("cbT")], lambda: nc.gpsimd.tensor_tensor(out=hs.rearrange("p (h x) -> p h x", h=8), in0=hst.rearrange("p (h x) -> p h x", h=8), in1=bc(SM128[:, sl, 8 * g:8 * g + 8].unsqueeze(2), [128, 8, 64]), op=ALU.mult))
                    V([tkq("hs"), ka], [hk], lambda: nc.vector.tensor_tensor(out=hst, in0=hs, in1=Aq[:, 0:512], op=ALU.add))
                    yield

                def ssd_proj(g):
                    wz0, kz0 = wnext()
                    for hlf in range(2):
                        bk = proj_b(wz0, kz0, hlf)
                        A(["b%d" % bk, ZS], [("zS", bs, hlf)], lambda: nc.scalar.activation(out=zS[:, hlf, :], in_=bank(bk)[:, 0:T], func=AF.Silu))
                        yield
                    wz1, kz1 = wnext()
                    for hlf in range(2):
                        bk = proj_b(wz1, kz1, hlf)
                        A(["b%d" % bk, ZS], [("zS", bs, 2 + hlf)], lambda: nc.scalar.activation(out=zS[:, 2 + hlf, :], in_=bank(bk)[:, 0:T], func=AF.Silu))
                        yield
                    cts = [4 * g, 4 * g + 1, 4 * g + 2, 4 * g + 3, 16 + g, 20 + g]
                    for ci2 in range(3):
                        wb, kb = wnext()
                        for hlf in range(2):
                            i = ci2 * 2 + hlf
                            bk = proj_b(wb, kb, hlf)
                            load_pre(i, bk, cts[i], tailA, "tailA")
                            conv_tile(i, convwA, convbA, cts[i], tailA, "tailA")
                            yield

                def ssd_epi(g):
                    yk = [("ySB", 0), ("ySB", 1)]
                    for i in range(4):
                        V(yk + [("post", bs, i), "D_fm"], yk, lambda: nc.vector.scalar_tensor_tensor(out=ySB[:, i, :], in0=post[:, i, :], scalar=D_fm[:, 4 * g + i:4 * g + i + 1], in1=ySB[:, i, :], op0=ALU.mult, op1=ALU.add))
                    G(yk + [("zS", bs, i) for i in range(4)], yk, lambda: nc.gpsimd.tensor_tensor(out=EP[:, 0:4 * T], in0=EP[:, 0:4 * T], in1=Rf[:, o2:o2 + 4 * T], op=ALU.mult))
                    A(yk, ["sqE"], lambda: nc.scalar.activation(out=r32(SQ[:, 0:4 * T]), in_=EP[:, 0:4 * T], func=AF.Square))
                    def f():
                        r = None
                        for i in range(4):
                            r = nc.tensor.matmul(bank(0)[:, 0:T], lhsT=r32(onesF[:, :]), rhs=r32(sqE[:, i, :]), start=(i == 0), stop=(i == 3))
                        return r
                    pm["i"] = 1
                    PE(["sqE", "onesF"], ["b0"], f)
                    A(["b0"], ["rnE"], lambda: nc.scalar.activation(out=rnE, in_=bank(0)[:, 0:T], func=AF.Ln, scale=1.0 / 512.0, bias=EPS))
                    A(["rnE"], ["rnE"], lambda: nc.scalar.activation(out=rnE, in_=rnE, func=AF.Exp, scale=-0.5))
                    for i in range(4):
                        V(yk + ["rnE", "normA_fm"], ["yaT"], lambda: nc.vector.scalar_tensor_tensor(out=yaT[:, 4 * g + i, 0:T], in0=ySB[:, i, :], scalar=normA_fm[:, 4 * g + i:4 * g + i + 1], in1=rnE, op0=ALU.mult, op1=ALU.mult))

                def gdn_chunk(j, q, c):
                    h0 = 2 * j
                    sl = q * nchg + c
                    cols = slice(q * W + c * Lg, q * W + (c + 1) * Lg)
                    sk = ("smg", sl)
                    ia, ib, ic = ((2, 3, 4), (7, 6, 5))[q]
                    Aq, Bq, Cq = bank(ia), bank(ib), bank(ic)
                    ka, kb_, kc_ = "b%d" % ia, "b%d" % ib, "b%d" % ic
                    TR = CTR_[q]; TN = CTN_[q]
                    tkq = lambda nm: ("ct", q, nm)
                    L2 = 2 * Lg
                    def tR(o, n, p=Lg):
                        return TR[0:p, o:o + n].rearrange("p (h x) -> p h x", h=2)
                    def tN(o, n, p=Lg):
                        return TN[0:p, o:o + n].rearrange("p (h x) -> p h x", h=2)
                    kbg = tR(0, 256); kd = tR(256, 256); vb = tR(512, 256)
                    Pb = [tR(768, L2), tR(1024, L2)]
                    PR = [tR(1280, 4 * Lg), tR(1792, 4 * Lg)]
                    RTf = tR(2304, L2)
                    QKT = tR(2560, L2); wv = tR(1280, 256)
                    AT = tN(0, L2); Dm = tN(256, L2); egc = tN(512, L2, 128); qg = tN(768, L2, 128)
                    nbm = tN(1024, L2); QK = tN(1280, L2); nwT = tN(1536, L2, 128); ss = TN[:, 1792:2048].rearrange("p (h x) -> p h x", h=2)
                    Sst = Ss[q][:, h0:h0 + 2, :]
                    skey = ("S", q, j)
                    def f():
                        r = None
                        for h in range(2):
                            r = nc.tensor.transpose(Aq[0:Lg, h * 128:(h + 1) * 128], post[:, 2 + h, cols], identF[:, :])
                        for h in range(2):
                            r = nc.tensor.transpose(Aq[0:Lg, 256 + h * 128:256 + (h + 1) * 128], post[:, 4 + h, cols], identF[:, :])
                        return r
                    PE([("post", bs, i) for i in range(2, 6)] + ["identF"], [ka, ka], f)
                    kps = Aq[0:Lg, 0:256].rearrange("p (h x) -> p h x", h=2)
                    vps = Aq[0:Lg, 256:512].rearrange("p (h x) -> p h x", h=2)
                    V([ka, sk], [tkq("kbg")], lambda: nc.vector.tensor_tensor(out=r32(kbg), in0=kps, in1=bc(SMG[0:Lg, sl, 64 + h0:66 + h0].unsqueeze(2), [Lg, 2, 128]), op=ALU.mult))
                    V([ka, sk], [tkq("kd")], lambda: nc.vector.tensor_tensor(out=r32(kd), in0=kps, in1=bc(SMG[0:Lg, sl, 48 + h0:50 + h0].unsqueeze(2), [Lg, 2, 128]), op=ALU.mult))
                    V([ka, sk], [tkq("vb")], lambda: nc.vector.tensor_tensor(out=r32(vb), in0=vps, in1=bc(SMG[0:Lg, sl, 0 + h0:2 + h0].unsqueeze(2), [Lg, 2, 128]), op=ALU.mult))
                    G([sk, "triu"], [tkq("AT")], lambda: nc.gpsimd.tensor_tensor(out=AT, in0=bc(SMG[0:Lg, sl, 16 + h0:18 + h0].unsqueeze(2), [Lg, 2, Lg]), in1=bc(triu[0:Lg, 0:Lg].unsqueeze(1), [Lg, 2, Lg]), op=ALU.mult))
                    G([sk, "mask01s"], [tkq("nbm")], lambda: nc.gpsimd.tensor_tensor(out=nbm, in0=bc(SMG[0:Lg, sl, 80 + h0:82 + h0].unsqueeze(2), [Lg, 2, Lg]), in1=bc(mask01s[0:Lg, 0:Lg].unsqueeze(1), [Lg, 2, Lg]), op=ALU.mult))
                    def f():
                        r = nc.tensor.matmul(Bq[:, 0:L2], lhsT=onesF[0:Lg, :], rhs=TN[0:Lg, 0:L2], start=True, stop=True)
                        for h in range(2):
                            r = nc.tensor.matmul(Bq[0:Lg, 256 + h * Lg:256 + (h + 1) * Lg], lhsT=r32(post[:, 2 + h, cols]), rhs=r32(post[:, 2 + h, cols]), start=True, stop=True)
                        for h in range(2):
                            r = nc.tensor.matmul(Cq[0:Lg, h * Lg:(h + 1) * Lg], lhsT=r32(post[:, h, cols]), rhs=r32(post[:, 2 + h, cols]), start=True, stop=True)
                        return r
                    PE([tkq("AT"), "onesF"] + [("post", bs, i) for i in range(4)], [kb_, kc_], f)
                    yield
                    V([kb_, sk], [tkq("Dm")], lambda: nc.vector.tensor_tensor(out=Dm, in0=bc(SMG[0:Lg, sl, 32 + h0:34 + h0].unsqueeze(2), [Lg, 2, Lg]), in1=Bq[0:Lg, 0:L2].rearrange("p (h x) -> p h x", h=2), op=ALU.subtract))
                    G([tkq("Dm"), "maskL"], [tkq("Dm")], lambda: nc.gpsimd.tensor_tensor(out=Dm, in0=Dm, in1=bc(maskL[0:Lg, 0:Lg].unsqueeze(1), [Lg, 2, Lg]), op=ALU.add))
                    A([tkq("Dm")], [tkq("Dm")], lambda: nc.scalar.activation(out=TN[0:Lg, 256:256 + L2], in_=TN[0:Lg, 256:256 + L2], func=AF.Exp))
                    A([kb_], [tkq("egc")], lambda: nc.scalar.activation(out=TN[:, 512:512 + L2], in_=Bq[:, 0:L2], func=AF.Exp))
                    G([tkq("egc"), ("post", bs, 0), ("post", bs, 1)], [tkq("qg")], lambda: nc.gpsimd.tensor_tensor(out=qg, in0=post[:, 0:2, cols], in1=egc, op=ALU.mult))
                    V([kb_, tkq("Dm")], [tkq("P0")], lambda: nc.vector.tensor_tensor(out=r32(Pb[0]), in0=Bq[0:Lg, 256:256 + L2].rearrange("p (h x) -> p h x", h=2), in1=Dm, op=ALU.mult))
                    G([tkq("P0"), tkq("nbm")], [tkq("P0")], lambda: nc.gpsimd.tensor_tensor(out=r32(Pb[0]), in0=Pb[0], in1=nbm, op=ALU.mult))
                    V([kc_, tkq("Dm")], [tkq("QK")], lambda: nc.vector.tensor_tensor(out=QK, in0=Cq[0:Lg, 0:L2].rearrange("p (h x) -> p h x", h=2), in1=Dm, op=ALU.mult))
                    def f():
                        r = None
                        for h in range(2):
                            r = nc.tensor.transpose(Cq[0:Lg, h * Lg:(h + 1) * Lg], Pb[0][:, h, :], identF[0:Lg, 0:Lg])
                        for h in range(2):
                            r = nc.tensor.transpose(Cq[0:Lg, 256 + h * Lg:256 + (h + 1) * Lg], QK[:, h, :], identF[0:Lg, 0:Lg])
                        return r
                    PE([tkq("P0"), tkq("QK"), "identF"], [kc_], f)
                    yield
                    A([kc_], [tkq("PR0")], lambda: nc.scalar.copy(out=r32(PR[0][:, :, 0:Lg]), in_=Cq[0:Lg, 0:L2].rearrange("p (h x) -> p h x", h=2)))
                    A([kc_], [tkq("QKT")], lambda: nc.scalar.copy(out=r32(QKT), in_=Cq[0:Lg, 256:256 + L2].rearrange("p (h x) -> p h x", h=2)))
                    V(["identF"], [tkq("PR0")], lambda: nc.vector.tensor_copy(out=r32(PR[0][:, :, Lg:2 * Lg]), in_=bc(identF[0:Lg, 0:Lg].unsqueeze(1), [Lg, 2, Lg])))
                    cur = 0
                    for lv in range(LVG):
                        nxt = 1 - cur
                        last = (lv == LVG - 1)
                        P, Pn = Pb[cur], Pb[nxt]
                        PRc, PRn = PR[cur], PR[nxt]
                        kP, kPn = tkq("P%d" % cur), tkq("P%d" % nxt)
                        kPR, kPRn = tkq("PR%d" % cur), tkq("PR%d" % nxt)
                        def f():
                            r = None
                            for h in range(2):
                                r = nc.tensor.matmul(Cq[0:Lg, h * Lg:(h + 1) * Lg], lhsT=r32(PRc[:, h, 0:Lg]), rhs=r32(P[:, h, :]), start=True, stop=True)
                            for h in range(2):
                                if last:
                                    r = nc.tensor.matmul(Bq[0:Lg, h * 2 * Lg + Lg:(h + 1) * 2 * Lg], lhsT=r32(P[:, h, :]), rhs=r32(PRc[:, h, Lg:2 * Lg]), start=True, stop=True)
                                else:
                                    r = nc.tensor.matmul(Bq[0:Lg, h * 2 * Lg:(h + 1) * 2 * Lg], lhsT=r32(P[:, h, :]), rhs=r32(PRc[:, h, :]), start=True, stop=True)
                            return r
                        PE([kP, kPR], [kc_, kb_], f)
                        yield
                        A([kc_], [kPn], lambda: nc.scalar.copy(out=r32(Pn), in_=Cq[0:Lg, 0:L2].rearrange("p (h x) -> p h x", h=2)))
                        Bv = Bq[0:Lg, 0:4 * Lg].rearrange("p (h x) -> p h x", h=2)
                        if not last:
                            V([kb_], [kPRn], lambda: nc.vector.tensor_copy(out=r32(PRn[:, :, 0:Lg]), in_=Bv[:, :, 0:Lg]))
                        V([kb_, kPR], [kPRn], lambda: nc.vector.tensor_tensor(out=r32(PRn[:, :, Lg:2 * Lg]), in0=Bv[:, :, Lg:2 * Lg], in1=PRc[:, :, Lg:2 * Lg], op=ALU.add))
                        cur = nxt
                    P = Pb[cur]; PRc = PR[cur]
                    def f():
                        r = None
                        for h in range(2):
                            r = nc.tensor.matmul(Bq[0:Lg, h * Lg:(h + 1) * Lg], lhsT=r32(P[:, h, :]), rhs=r32(PRc[:, h, Lg:2 * Lg]), start=True, stop=True)
                        return r
                    PE([tkq("P%d" % cur), tkq("PR%d" % cur)], [kb_], f)
                    yield
                    RT = RTf; kRT = tkq("RTf")
                    V([kb_, tkq("PR%d" % cur)], [kRT], lambda: nc.vector.tensor_tensor(out=r32(RTf), in0=Bq[0:Lg, 0:L2].rearrange("p (h x) -> p h x", h=2), in1=PRc[:, :, Lg:2 * Lg], op=ALU.add))
                    def f():
                        r = None
                        for h in range(2):
                            r = nc.tensor.matmul(Aq[:, h * Lg:(h + 1) * Lg], lhsT=r32(kbg[:, h, :]), rhs=r32(RT[:, h, :]), start=True, stop=True)
                        return r
                    PE([tkq("kbg"), kRT], [ka], f)
                    yield
                    V([ka], [tkq("nwT")], lambda: nc.vector.tensor_scalar(out=nwT, in0=Aq[:, 0:L2].rearrange("p (h x) -> p h x", h=2), scalar1=-1.0, scalar2=None, op0=ALU.mult))
                    def f():
                        r = None
                        for h in range(2):
                            nc.tensor.matmul(Bq[0:Lg, h * 128:(h + 1) * 128], lhsT=r32(RT[:, h, :]), rhs=r32(vb[:, h, :]), start=True, stop=False)
                            r = nc.tensor.matmul(Bq[0:Lg, h * 128:(h + 1) * 128], lhsT=nwT[:, h, :], rhs=Sst[:, h, :], start=False, stop=True)
                        return r
                    PE([kRT, tkq("vb"), tkq("nwT"), skey], [kb_], f)
                    yield
                    V([kb_], [tkq("PR0")], lambda: nc.vector.tensor_copy(out=r32(wv), in_=Bq[0:Lg, 0:256].rearrange("p (h x) -> p h x", h=2)))
                    def f():
                        r = None
                        for h in range(2):
                            nc.tensor.matmul(Cq[:, 256 + h * Lg:256 + (h + 1) * Lg], lhsT=Sst[:, h, :], rhs=qg[:, h, :], start=True, stop=False)
                            r = nc.tensor.matmul(Cq[:, 256 + h * Lg:256 + (h + 1) * Lg], lhsT=r32(wv[:, h, :]), rhs=r32(QKT[:, h, :]), start=False, stop=True)
                        for h in range(2):
                            r = nc.tensor.matmul(Aq[:, 256 + h * 128:256 + (h + 1) * 128], lhsT=r32(kd[:, h, :]), rhs=r32(wv[:, h, :]), start=True, stop=True)
                        return r
                    PE([skey, tkq("qg"), tkq("PR0"), tkq("QKT"), tkq("kd")], [kc_, ka], f)
                    yield
                    A([kc_], [("ySB", q)], lambda: nc.scalar.copy(out=ySB[:, 0:2, cols], in_=Cq[:, 256:256 + L2].rearrange("p (h x) -> p h x", h=2)))
                    G([skey, ("smg128", sl)], [tkq("ss")], lambda: nc.gpsimd.tensor_tensor(out=ss, in0=Sst, in1=bc(SMG128[:, sl, h0:h0 + 2].unsqueeze(2), [128, 2, 128]), op=ALU.mult))
                    V([tkq("ss"), ka], [skey], lambda: nc.vector.tensor_tensor(out=Sst, in0=ss, in1=Aq[:, 256:512].rearrange("p (h x) -> p h x", h=2), op=ALU.add))
                    yield

                def gdn_proj(j):
                    for part in range(3):
                        wb, kb = wnext()
                        for hlf in range(2):
                            i = part * 2 + hlf
                            ct = part * 16 + 2 * j + hlf
                            bk = proj_b(wb, kb, hlf)
                            load_pre(i, bk, ct, tailB, "tailB")
                            conv_tile(i, convwB, None, ct, tailB, "tailB")
                            yield
                    wb, kb = wnext()
                    for hlf in range(2):
                        bk = proj_b(wb, kb, hlf)
                        A(["b%d" % bk, ZS], [("zS", bs, hlf)], lambda: nc.scalar.activation(out=zS[:, hlf, :], in_=bank(bk)[:, 0:T], func=AF.Silu))
                        yield
                    pk4 = [("post", bs, i) for i in range(4)]
                    A(pk4, ["sqE"], lambda: nc.scalar.activation(out=r32(SQ[:, 0:4 * T]), in_=POSTS[bs][:, 0:4 * T], func=AF.Square))
                    for i in range(4):
                        bk = pm["i"] % 2
                        pm["i"] += 1
                        PE(["sqE", "onesF"], ["b%d" % bk], lambda: nc.tensor.matmul(bank(bk)[:, 0:T], lhsT=r32(onesF[:, :]), rhs=r32(sqE[:, i, :]), start=True, stop=True))
                        A(["b%d" % bk], ["rnE"], lambda: nc.scalar.activation(out=rnE, in_=bank(bk)[:, 0:T], func=AF.Ln, bias=EPS))
                        A(["rnE"], ["rnE"], lambda: nc.scalar.activation(out=rnE, in_=rnE, func=AF.Exp, scale=-0.5))
                        scl = (128.0 ** -0.5) if i < 2 else 1.0
                        V(["rnE", ("post", bs, i)], [("post", bs, i)], lambda: nc.vector.scalar_tensor_tensor(out=r32(post[:, i, :]), in0=post[:, i, :], scalar=scl, in1=rnE, op0=ALU.mult, op1=ALU.mult))
                        yield

                def gdn_epi(j):
                    ok_ = [("ySB", 0), ("ySB", 1)]
                    A(ok_, ["sqE"], lambda: nc.scalar.activation(out=r32(SQ[:, 0:2 * T]), in_=EP[:, 0:2 * T], func=AF.Square))
                    for h in range(2):
                        bk = pm["i"] % 2
                        pm["i"] += 1
                        PE(["sqE", "onesF"], ["b%d" % bk], lambda: nc.tensor.matmul(bank(bk)[:, 0:T], lhsT=r32(onesF[:, :]), rhs=r32(sqE[:, h, :]), start=True, stop=True))
                        A(["b%d" % bk], ["rnE"], lambda: nc.scalar.activation(out=rnE, in_=bank(bk)[:, 0:T], func=AF.Ln, scale=1.0 / 128.0, bias=EPS))
                        A(["rnE"], ["rnE"], lambda: nc.scalar.activation(out=rnE, in_=rnE, func=AF.Exp, scale=-0.5))
                        V(ok_ + ["rnE", "normB_col"], ok_, lambda: nc.vector.scalar_tensor_tensor(out=ySB[:, h, :], in0=ySB[:, h, :], scalar=normB_col[:, 0:1], in1=rnE, op0=ALU.mult, op1=ALU.mult))
                        V(ok_ + [("zS", bs, h)], ["obT"], lambda: nc.vector.tensor_tensor(out=obT[:, 2 * j + h, 0:T], in0=ySB[:, h, :], in1=zS[:, h, :], op=ALU.mult))

                return {"ssd_proj": ssd_proj, "ssd_chunk": ssd_chunk, "ssd_epi": ssd_epi, "gdn_proj": gdn_proj, "gdn_chunk": gdn_chunk, "gdn_epi": gdn_epi}

            sets = [make_set(0, R1), make_set(1, R2)]
            phases = [("ssd", g) for g in range(4)] + [("gdn", j) for j in range(8)]

            def proj_of(k):
                kind, idx = phases[k]
                return sets[k % 2][kind + "_proj"](idx)

            def drain(gen):
                for _ in gen:
                    pass

            def run_mixed(gens, pg):
                live = list(gens)
                while live:
                    for gq in list(live):
                        try:
                            next(gq)
                        except StopIteration:
                            live.remove(gq)
                    if pg is not None:
                        try:
                            next(pg)
                        except StopIteration:
                            pg = None
                return pg

            drain(proj_of(0))
            for k in range(12):
                kind, idx = phases[k]
                S_ = sets[k % 2]
                if k + 1 < 12:
                    drain(proj_of(k + 1))
                for c in range(nch if kind == "ssd" else nchg):
                    run_mixed([S_[kind + "_chunk"](idx, 0, c), S_[kind + "_chunk"](idx, 1, c)], None)
                S_[kind + "_epi"](idx)
            checkpoint("gdn")
            nb = len(blocks)
            bn = [sum(s[2] for s in blk[1]) for blk in blocks]
            BAR(["pre", "post", "zS"] + [("pre", 0, i) for i in range(6)] + [("cacc", 0, i) for i in range(2)] + [("zS", 0, i) for i in range(4)], R1_KEYS)
            load_x()
            sa = EP[:, 0:4 * 512].rearrange("p (b c) -> p b c", c=512)
            sbv = EP[:, 2048:4096].rearrange("p (b c) -> p b c", c=512)
            EPK = [("ySB", 0), ("ySB", 1), "sqE", "rnE"]
            BAR(EPK, ["sa", "sb"])
            for cg in range(4):
                def acc_group(bank0, srcT, srckey, nchunks, kc0):
                    for chn in range(nchunks):
                        wb, kb = wnext()
                        wv_ = wb[:, :].rearrange("p (k c) -> p k c", c=512)
                        def f():
                            r = None
                            for b in range(nb):
                                col0 = blocks[b][0]
                                for kc in range(8):
                                    kk = kc0 + chn * 8 + kc
                                    r = nc.tensor.matmul(bank(bank0 + b)[0:bn[b], :], lhsT=srcT[:, kk, col0:col0 + bn[b]], rhs=wv_[:, kc, :], start=(chn == 0 and kc == 0), stop=(chn == nchunks - 1 and kc == 7))
                            return r
                        PE([srckey, kb], ["b%d" % (bank0 + b) for b in range(nb)], f)
                g0, g1, g2, g3 = (0, 2, 4, 6) if nb <= 2 else (0, 4, 0, 4)
                acc_group(g0, uT, "uT", 2, 0)
                for b in range(nb):
                    A(["b%d" % (g0 + b)], ["sa"], lambda: nc.scalar.activation(out=sa[0:bn[b], b, :], in_=bank(g0 + b)[0:bn[b], :], func=AF.Sigmoid))
                acc_group(g1, uT, "uT", 2, 0)
                for b in range(nb):
                    A(["b%d" % (g1 + b)], ["sb"], lambda: nc.scalar.activation(out=sbv[0:bn[b], b, :], in_=bank(g1 + b)[0:bn[b], :], func=AF.Sigmoid))
                acc_group(g2, yaT, "yaT", 2, 0)
                for b in range(nb):
                    V(["b%d" % (g2 + b), "sa"], ["sa"], lambda: nc.vector.tensor_tensor(out=sa[0:bn[b], b, :], in0=sa[0:bn[b], b, :], in1=bank(g2 + b)[0:bn[b], :], op=ALU.mult))
                acc_group(g3, obT, "obT", 2, 0)
                for b in range(nb):
                    V(["b%d" % (g3 + b), "sb"], ["sb"], lambda: nc.vector.tensor_tensor(out=sbv[0:bn[b], b, :], in0=sbv[0:bn[b], b, :], in1=bank(g3 + b)[0:bn[b], :], op=ALU.mult))
                for b in range(nb):
                    G(["sa", "sb"], ["sa"], lambda: nc.gpsimd.tensor_tensor(out=sa[0:bn[b], b, :], in0=sa[0:bn[b], b, :], in1=sbv[0:bn[b], b, :], op=ALU.add))
                    G(["sa", "xt"], ["xt"], lambda: nc.gpsimd.tensor_tensor(out=xt[0:bn[b], b, cg * 512:(cg + 1) * 512], in0=xt[0:bn[b], b, cg * 512:(cg + 1) * 512], in1=sa[0:bn[b], b, :], op=ALU.add))

            checkpoint("C")
            norm_to_T(normmlp_fm, "normmlp_fm")
            hid = [yaT, obT]; hkey = ["yaT", "obT"]
            rl = EP[:, 0:512]
            pm["i"] = 0
            for fb in range(4):
                hT_ = hid[fb % 2]; hk_ = hkey[fb % 2]
                for c8 in range(8):
                    wb, kb = wnext()
                    for hlf in range(2):
                        bk = proj_b(wb, kb, hlf, (0, 1, 4, 7))
                        A(["b%d" % bk, "sa"], ["sa"], lambda: nc.scalar.activation(out=rl[:, 0:T], in_=bank(bk)[:, 0:T], func=AF.Relu))
                        G(["sa"], [hk_], lambda: nc.gpsimd.tensor_tensor(out=hT_[:, c8 * 2 + hlf, 0:T], in0=rl[:, 0:T], in1=rl[:, 0:T], op=ALU.mult))
                for cg in range(4):
                    b0 = 2 + 3 * (cg % 2) if nb <= 3 else None
                    base = (2 if cg % 2 == 0 else 5) if nb <= 3 else None
                    if nb <= 3:
                        bl = [base + b for b in range(nb)]
                    else:
                        bl = [2, 3, 4, 5] if cg % 2 == 0 else [6, 7, 0, 1]
                    for kh in range(2):
                        wb, kb = wnext()
                        wv_ = wb[:, :].rearrange("p (k c) -> p k c", c=512)
                        def f():
                            r = None
                            for b in range(nb):
                                col0 = blocks[b][0]
                                for kc in range(8):
                                    r = nc.tensor.matmul(bank(bl[b])[0:bn[b], :], lhsT=hT_[:, kh * 8 + kc, col0:col0 + bn[b]], rhs=wv_[:, kc, :], start=(kh == 0 and kc == 0), stop=(kh == 1 and kc == 7))
                            return r
                        PE([hk_, kb], ["b%d" % x for x in bl[:nb]], f)
                    for b in range(nb):
                        V(["b%d" % bl[b], "xt"], ["xt"], lambda: nc.vector.tensor_tensor(out=xt[0:bn[b], b, cg * 512:(cg + 1) * 512], in0=xt[0:bn[b], b, cg * 512:(cg + 1) * 512], in1=bank(bl[b])[0:bn[b], :], op=ALU.add))
            pm["i"] = 0

            checkpoint("D")
            for b, (col0, srcs, outs) in enumerate(blocks):
                n = bn[b]
                if not outs:
                    continue
                A(["xt"], ["xn", "stat"], lambda: nc.scalar.activation(out=xn[0:n, :], in_=xt[0:n, b, :], func=AF.Square, accum_out=stat[0:n, 4:5]))
                A(["stat"], ["stat"], lambda: nc.scalar.activation(out=stat[0:n, 5:6], in_=stat[0:n, 4:5], func=AF.Ln, scale=1.0 / D, bias=EPS))
                A(["stat"], ["stat"], lambda: nc.scalar.activation(out=stat[0:n, 6:7], in_=stat[0:n, 5:6], func=AF.Exp, scale=-0.5))
                V(["xt", "stat", "normf_bc"], ["xt"], lambda: nc.vector.scalar_tensor_tensor(out=xt[0:n, b, :], in0=xt[0:n, b, :], scalar=stat[0:n, 6:7], in1=normf_bc[0:n, :], op0=ALU.mult, op1=ALU.mult))
                for (dst, p0, nn) in outs:
                    tk.dma(dst, xt[p0:p0 + nn, b, :], ["xt"], [])

        def zero_states():
            for q in range(2):
                V([], [("hT", q, g) for g in range(4)], lambda: nc.vector.memset(hTs[q][:, :], 0.0))
                G([], [("S", q, j) for j in range(8)], lambda: nc.gpsimd.memset(Ss[q][:, :, :], 0.0))
            G([], ["tailA"], lambda: nc.gpsimd.memset(tailA[:, :, :], 0.0))
            G([], ["tailB"], lambda: nc.gpsimd.memset(tailB[:, :, :], 0.0))

        def load_states(s0):
            for q in range(2):
                stg = EP[:, 0:2048].rearrange("p (c n) -> p c n", n=128)
                tk.dma(stg, sA[s0 + q].rearrange("(c r) n -> r c n", r=128), [], ["EPst"])
                for c4 in range(4):
                    def f():
                        r = None
                        for i in range(4):
                            r = nc.tensor.transpose(bank(2)[:, i * 128:(i + 1) * 128], stg[:, c4 * 4 + i, :], identF[:, :])
                        return r
                    PE(["EPst", "identF"], ["b2"], f)
                    V(["b2"], [("hT", q, c4)], lambda: nc.vector.tensor_copy(out=hTs[q][:, c4 * 512:(c4 + 1) * 512], in_=bank(2)[:, 0:512]))
                tk.dma(Ss[q][:, :, :], sB[s0 + q].rearrange("(h k) v -> k h v", k=128), [], [("S", q, j) for j in range(8)])
            feat_from_rows(sconvA[s0:s0 + 2].rearrange("s r c -> (s r) c"), 6, 3072, tailA[:, :, :], "tailA")
            feat_from_rows(sconvB[s0:s0 + 2].rearrange("s r c -> (s r) c"), 6, 6144, tailB[:, :, :], "tailB")

        def store_states(o_sA, o_sB, o_cA, o_cB, s0):
            for q in range(2):
                stg = EP[:, 0:2048].rearrange("p (c n) -> p c n", n=128)
                for c4 in range(4):
                    def f():
                        r = None
                        for i in range(4):
                            r = nc.tensor.transpose(bank(2)[:, i * 128:(i + 1) * 128], hTs[q][:, (c4 * 4 + i) * 128:(c4 * 4 + i + 1) * 128], identF[:, :])
                        return r
                    PE([("hT", q, c4), "identF"], ["b2"], f)
                    V(["b2"], ["EPst"], lambda: nc.vector.tensor_copy(out=stg[:, c4 * 4:(c4 + 1) * 4, :], in_=bank(2)[:, 0:512].rearrange("p (c n) -> p c n", n=128)))
                tk.dma(o_sA[s0 + q].rearrange("(c r) n -> r c n", r=128), stg, ["EPst"], [])
                tk.dma(o_sB[s0 + q].rearrange("(h k) v -> k h v", k=128), Ss[q][:, :, :], [("S", q, j) for j in range(8)], [])
            rows_from_feat(tailA, "tailA", 6, 3072, o_cA[s0:s0 + 2].rearrange("s r c -> (s r) c"))
            rows_from_feat(tailB, "tailB", 6, 6144, o_cB[s0:s0 + 2].rearrange("s r c -> (s r) c"))

        n_tiles = 2 + 1 + 32 // NCH
        wstate["total"] = n_tiles * NCHUNK

        def ep_barrier():
            BAR(EP_ALL, EP_ALL)

        def schedule():
            checkpoint("prologue")
            for st in range(2):
                s0 = 2 * st
                ep_barrier()
                load_states(s0)
                checkpoint("states")
                ep_barrier()
                blocks = [(0, [(xs[s0], 0, 64), (xs[s0 + 1], 64, 64)], [(ys[s0], 0, 64), (ys[s0 + 1], 64, 64)])]
                process_tile(1, 64, blocks, None, "S")
                ep_barrier()
                store_states(o_sA_s, o_sB_s, o_convA_s, o_convB_s, s0)
                checkpoint("tile0")
            ep_barrier()
            zero_states()
            blocks = [(0, [(meta[:, :], 0, 16), (meta[:, :], 16, 16)], [])]
            process_tile(1, 16, blocks, None, "M")
            if stop == "meta":
                ep_barrier()
                store_states(o_sA_p, o_sB_p, o_convA_p, o_convB_p, 0)
                checkpoint("meta")
            Wp = NCH * 64
            for ti in range(32 // NCH):
                t0 = ti * Wp
                blocks = []
                for q in range(2):
                    for bb in range(Wp // 128):
                        r0 = t0 + bb * 128
                        blocks.append((q * Wp + bb * 128, [(xp[q, r0:r0 + 128, :], 0, 128)], [(yp[q, r0:r0 + 128, :], 0, 128)]))
                process_tile(NCH * 64 // 128, 128, blocks, None, "P")
                if stop == "ptile0" or (stop == "ptile1" and ti == 1):
                    ep_barrier()
                    store_states(o_sA_p, o_sB_p, o_convA_p, o_convB_p, 0)
                    raise _Stop()
            ep_barrier()
            store_states(o_sA_p, o_sB_p, o_convA_p, o_convB_p, 0)

        try:
            schedule()
        except _Stop:
            pass
        tk.finish()
    return nc


_NC_CACHE = {}


def kernel(**inputs):
    f = lambda a: np.ascontiguousarray(np.asarray(a, dtype=np.float32))
    x_prompt = f(inputs["x_prompt"]); x_sample = f(inputs["x_sample"])
    shared = {
        "meta": f(inputs["meta_tokens"]),
        "w_in": f(inputs["w_in"][0]), "w_out": f(inputs["w_out"][0]), "w_up": f(inputs["w_up"][0]), "w_down": f(inputs["w_down"][0]),
        "p_normmix": f(inputs["norm_mix_w"]).reshape(1, D), "p_convwA": f(inputs["ssd_conv_w"][0]), "p_convbA": f(inputs["ssd_conv_b"]).reshape(1, 3072),
        "p_dtbA": f(inputs["ssd_dt_bias"]).reshape(32, 1), "p_alogA": f(inputs["ssd_a_log"]).reshape(1, 32), "p_dA": f(inputs["ssd_d"]).reshape(1, 32),
        "p_normA": f(inputs["ssd_norm_w"]).reshape(1, D), "p_convwB": f(inputs["gdn_conv_w"][0]),
        "p_dtbB": f(inputs["gdn_dt_bias"]).reshape(16, 1), "p_alogB": f(inputs["gdn_a_log"]).reshape(16, 1), "p_normB": f(inputs["gdn_norm_w"]).reshape(128, 1),
        "p_normmlp": f(inputs["norm_mlp_w"]).reshape(1, D), "p_normf": f(inputs["norm_f_w"]).reshape(1, D),
    }
    sca = f(inputs["state_ssd_conv"][0]); ssa = f(inputs["state_ssd"][0]).reshape(32, 2048, 128)
    scb = f(inputs["state_gdn_conv"][0]); ssb = f(inputs["state_gdn"][0]).reshape(32, 2048, 128)
    in_maps = []
    for c in range(8):
        m = dict(shared)
        m["xp"] = x_prompt[2 * c:2 * c + 2]
        m["xs"] = x_sample[4 * c:4 * c + 4]
        m["sconvA"] = sca[4 * c:4 * c + 4]; m["sA"] = ssa[4 * c:4 * c + 4]
        m["sconvB"] = scb[4 * c:4 * c + 4]; m["sB"] = ssb[4 * c:4 * c + 4]
        in_maps.append(m)
    if "nc" not in _NC_CACHE:
        _NC_CACHE["nc"] = build_program()
    nc = _NC_CACHE["nc"]
    res = run_bass_kernel_spmd(nc, in_maps, core_ids=list(range(8)))
    R = res.results
    cat = lambda k: np.concatenate([np.asarray(r[k], dtype=np.float32) for r in R], axis=0)
    y_prompt = cat("yp"); y_sample = cat("ys")
    outs = [y_prompt, y_sample,
            cat("o_convA_p")[None], cat("o_sA_p").reshape(1, 16, 32, 64, 128),
            cat("o_convB_p")[None], cat("o_sB_p").reshape(1, 16, 16, 128, 128),
            cat("o_convA_s")[None], cat("o_sA_s").reshape(1, 32, 32, 64, 128),
            cat("o_convB_s")[None], cat("o_sB_s").reshape(1, 32, 16, 128, 128)]
    return tuple(outs)
```
